# Optimizing a Trainium2 kernel written in Bass

```python
import math
import jax, jax.numpy as jnp
from jax import lax
import numpy as np

D_MODEL = 1024
BATCH = 8
SEQ = 2048
DEPTH = 1
DEC_BATCH = 128
DEC_SEQ = 1
PAST_LEN = 16384
PAGE_SIZE = 128

D_MIX = D_MODEL
HG_WIDTH = D_MIX // 2
GDN_WIDTH = D_MIX - HG_WIDTH
HG_HEADS = 4
HG_KEY = 128
HG_VAL = HG_WIDTH // HG_HEADS
GDN_HEADS = 4
GDN_DK = 128
GDN_DV = GDN_WIDTH // GDN_HEADS
CONV_W = 4
CHUNK = 64
EPS = 1e-6

SPLITS = (HG_HEADS * HG_KEY, HG_HEADS * HG_KEY, HG_WIDTH, HG_WIDTH,
          GDN_HEADS * GDN_DK, GDN_HEADS * GDN_DK, GDN_WIDTH, GDN_WIDTH,
          GDN_HEADS, GDN_HEADS)
D_IN = sum(SPLITS)
SPLIT_POINTS = tuple(int(v) for v in np.cumsum(SPLITS)[:-1])
CONV_CH = 2 * GDN_HEADS * GDN_DK + GDN_WIDTH

kernel_name = "hgrn2_gated_deltanet_parallel_heads_step"


def rmsnorm(x, w):
    x32 = x.astype(jnp.float32)
    y = x32 * lax.rsqrt(jnp.mean(x32 * x32, axis=-1, keepdims=True) + EPS)
    return (y * w.astype(jnp.float32)).astype(x.dtype)


def l2norm(x):
    return x * lax.rsqrt(jnp.sum(x * x, axis=-1, keepdims=True) + EPS)


def _chunks(a):
    B, T = a.shape[:2]
    a = a.reshape((B, T // CHUNK, CHUNK) + a.shape[2:])
    return jnp.moveaxis(jnp.moveaxis(a, 1, 0), 2, 3)


def _unchunks(o):
    n, B, H, C, V = o.shape
    return jnp.moveaxis(jnp.moveaxis(o, 0, 1), 3, 2).reshape(B, n * C, H, V)


def hgrn2_chunked(q, k, v, log_f, s0):
    causal = jnp.tril(jnp.ones((CHUNK, CHUNK), dtype=bool))

    def step(s, inp):
        qc, kc, vc, lfc = inp
        b = jnp.cumsum(lfc, axis=2)
        diff = b[:, :, :, None, :] - b[:, :, None, :, :]
        decay = jnp.exp(jnp.where(causal[:, :, None], diff, -jnp.inf))
        scores = jnp.einsum('bhtk,bhtsk,bhsk->bhts', qc, decay, kc)
        o = (jnp.einsum('bhts,bhsv->bhtv', scores, vc)
             + jnp.einsum('bhtk,bhkv->bhtv', qc * jnp.exp(b), s))
        b_last = b[:, :, -1:, :]
        s_new = (jnp.exp(b_last[:, :, 0, :])[..., None] * s
                 + jnp.einsum('bhsk,bhsv->bhkv', kc * jnp.exp(b_last - b), vc))
        return s_new, o

    s_fin, o = lax.scan(step, s0, (_chunks(q), _chunks(k), _chunks(v), _chunks(log_f)))
    return _unchunks(o), s_fin


def hgrn2_recurrent(q, k, v, log_f, s0):
    def step(s, inp):
        qt, kt, vt, lft = inp
        s = jnp.exp(lft)[..., None] * s + kt[..., None] * vt[:, :, None, :]
        return s, jnp.einsum('bhk,bhkv->bhv', qt, s)

    tm = lambda a: jnp.moveaxis(a, 1, 0)
    s_fin, o = lax.scan(step, s0, (tm(q), tm(k), tm(v), tm(log_f)))
    return jnp.moveaxis(o, 0, 1), s_fin


def gdn_chunked(q, k, v, beta, g, s0):
    causal = jnp.tril(jnp.ones((CHUNK, CHUNK), dtype=bool))
    strict = jnp.tril(jnp.ones((CHUNK, CHUNK), dtype=bool), k=-1)
    eye = jnp.eye(CHUNK, dtype=jnp.float32)
    V = v.shape[-1]

    def step(s, inp):
        qc, kc, vc, bc, gc = inp
        G = jnp.cumsum(gc, axis=-1)
        diff = G[..., :, None] - G[..., None, :]
        L = jnp.exp(jnp.where(causal, diff, -jnp.inf))
        kb = kc * bc[..., None]
        A = jnp.where(strict, jnp.einsum('bhtk,bhsk->bhts', kb, kc) * L, 0.0) + eye
        rhs = jnp.concatenate([vc * bc[..., None], kb * jnp.exp(G)[..., None]], axis=-1)
        sol = lax.linalg.triangular_solve(A, rhs, left_side=True, lower=True,
                                          unit_diagonal=True)
        u = sol[..., :V] - jnp.einsum('bhtk,bhkv->bhtv', sol[..., V:], s)
        attn = jnp.einsum('bhtk,bhsk->bhts', qc, kc) * L
        o = (jnp.einsum('bhtk,bhkv->bhtv', qc * jnp.exp(G)[..., None], s)
             + jnp.einsum('bhts,bhsv->bhtv', attn, u))
        G_last = G[..., -1:]
        s_new = (jnp.exp(G_last)[..., None] * s
                 + jnp.einsum('bhsk,bhsv->bhkv', kc * jnp.exp(G_last - G)[..., None], u))
        return s_new, o

    s_fin, o = lax.scan(step, s0, (_chunks(q), _chunks(k), _chunks(v),
                                   _chunks(beta), _chunks(g)))
    return _unchunks(o), s_fin


def gdn_recurrent(q, k, v, beta, g, s0):
    def step(s, inp):
        qt, kt, vt, bt, gt = inp
        s = jnp.exp(gt)[..., None, None] * s
        delta = (vt - jnp.einsum('bhk,bhkv->bhv', kt, s)) * bt[..., None]
        s = s + kt[..., None] * delta[:, :, None, :]
        return s, jnp.einsum('bhk,bhkv->bhv', qt, s)

    tm = lambda a: jnp.moveaxis(a, 1, 0)
    s_fin, o = lax.scan(step, s0, (tm(q), tm(k), tm(v), tm(beta), tm(g)))
    return jnp.moveaxis(o, 0, 1), s_fin


def hybrid_mixer(h, w_in, conv_w, lb, a_log, dt_bias, hg_norm, gdn_norm, w_out,
                 s_hg, s_gdn, conv_prev, chunked):
    f32 = jnp.float32
    B, T, _ = h.shape
    p = jnp.einsum('btd,de->bte', h, w_in).astype(f32)
    hq, hf, hi, hz, gq, gk, gv, gz, gb, ga = jnp.split(p, SPLIT_POINTS, axis=-1)

    fg = lb + (1.0 - lb) * jax.nn.sigmoid(hf)
    hg_q = hq.reshape(B, T, HG_HEADS, HG_KEY)
    hg_k = (1.0 - fg).reshape(B, T, HG_HEADS, HG_KEY)
    hg_lf = jnp.log(fg).reshape(B, T, HG_HEADS, HG_KEY)
    hg_v = hi.reshape(B, T, HG_HEADS, HG_VAL)

    qkv = jnp.concatenate([gq, gk, gv], axis=-1)
    padded = jnp.concatenate([conv_prev.astype(f32), qkv], axis=1)
    cw = conv_w.astype(f32)
    conv = sum(padded[:, j:j + T] * cw[j] for j in range(CONV_W))
    qkv_c = jax.nn.silu(conv)
    new_conv = padded[:, T:]
    cq, ck, cv = jnp.split(qkv_c, (GDN_HEADS * GDN_DK, 2 * GDN_HEADS * GDN_DK), axis=-1)
    gdn_q = l2norm(cq.reshape(B, T, GDN_HEADS, GDN_DK)) * (GDN_DK ** -0.5)
    gdn_k = l2norm(ck.reshape(B, T, GDN_HEADS, GDN_DK))
    gdn_v = cv.reshape(B, T, GDN_HEADS, GDN_DV)
    beta = jax.nn.sigmoid(gb)
    g = -jnp.exp(a_log.astype(f32)) * jax.nn.softplus(ga + dt_bias.astype(f32))

    s_hg = s_hg.astype(f32)
    s_gdn = s_gdn.astype(f32)
    if chunked:
        o_hg, s_hg_new = hgrn2_chunked(hg_q, hg_k, hg_v, hg_lf, s_hg)
        o_gdn, s_gdn_new = gdn_chunked(gdn_q, gdn_k, gdn_v, beta, g, s_gdn)
    else:
        o_hg, s_hg_new = hgrn2_recurrent(hg_q, hg_k, hg_v, hg_lf, s_hg)
        o_gdn, s_gdn_new = gdn_recurrent(gdn_q, gdn_k, gdn_v, beta, g, s_gdn)

    o_hg = rmsnorm(o_hg, hg_norm) * jax.nn.silu(hz.reshape(B, T, HG_HEADS, HG_VAL))
    o_gdn = rmsnorm(o_gdn, gdn_norm) * jax.nn.silu(gz.reshape(B, T, GDN_HEADS, GDN_DV))
    o = jnp.concatenate([o_hg.reshape(B, T, HG_WIDTH), o_gdn.reshape(B, T, GDN_WIDTH)], axis=-1)
    out = jnp.einsum('bte,ed->btd', o.astype(h.dtype), w_out)
    return out.astype(h.dtype), s_hg_new, s_gdn_new, new_conv


def setup_inputs(seed: int = 0) -> dict:
    key = jax.random.key(seed)
    ks = jax.random.split(key, 16)
    f32 = jnp.float32
    x_prompt = jax.random.normal(ks[0], (BATCH, SEQ, D_MODEL), f32)
    x_sample = jax.random.normal(ks[1], (DEC_BATCH, DEC_SEQ, D_MODEL), f32)
    state_hgrn = jax.random.normal(ks[2], (DEPTH, DEC_BATCH, HG_HEADS, HG_KEY, HG_VAL), f32) * 0.5
    state_gdn = jax.random.normal(ks[3], (DEPTH, DEC_BATCH, GDN_HEADS, GDN_DK, GDN_DV), f32) * (GDN_DK ** -0.5)
    state_gdn_conv = jax.random.normal(ks[4], (DEPTH, DEC_BATCH, CONV_W - 1, CONV_CH), f32)
    norm_w = 1.0 + 0.01 * jax.random.normal(ks[5], (DEPTH, D_MODEL), f32)
    w_in = jax.random.normal(ks[6], (DEPTH, D_MODEL, D_IN), f32) * (D_MODEL ** -0.5)
    hg_lb_logits = 0.1 * jax.random.normal(ks[7], (DEPTH + 1, HG_HEADS * HG_KEY), f32)
    conv_w = jax.random.normal(ks[8], (DEPTH, CONV_W, CONV_CH), f32) * (CONV_W ** -0.5)
    gdn_a_log = jnp.log(jax.random.uniform(ks[9], (DEPTH, GDN_HEADS), f32, 1.0, 16.0))
    dt = jnp.exp(jax.random.uniform(ks[10], (DEPTH, GDN_HEADS), f32, math.log(1e-3), math.log(1e-1)))
    gdn_dt_bias = dt + jnp.log(-jnp.expm1(-dt))
    hg_out_norm = 1.0 + 0.01 * jax.random.normal(ks[11], (DEPTH, HG_VAL), f32)
    gdn_out_norm = 1.0 + 0.01 * jax.random.normal(ks[12], (DEPTH, GDN_DV), f32)
    w_out = jax.random.normal(ks[13], (DEPTH, D_MIX, D_MODEL), f32) * (D_MIX ** -0.5)
    final_norm = 1.0 + 0.01 * jax.random.normal(ks[14], (D_MODEL,), f32)
    return {"x_prompt": x_prompt, "x_sample": x_sample,
            "state_hgrn": state_hgrn, "state_gdn": state_gdn, "state_gdn_conv": state_gdn_conv,
            "norm_w": norm_w, "w_in": w_in, "hg_lb_logits": hg_lb_logits, "conv_w": conv_w,
            "gdn_a_log": gdn_a_log, "gdn_dt_bias": gdn_dt_bias,
            "hg_out_norm": hg_out_norm, "gdn_out_norm": gdn_out_norm,
            "w_out": w_out, "final_norm": final_norm}


def reference(x_prompt, x_sample, state_hgrn, state_gdn, state_gdn_conv,
              norm_w, w_in, hg_lb_logits, conv_w, gdn_a_log, gdn_dt_bias,
              hg_out_norm, gdn_out_norm, w_out, final_norm):
    f32 = jnp.float32
    lower_bounds = jnp.cumsum(jax.nn.softmax(hg_lb_logits.astype(f32), axis=0), axis=0)
    hp, hs = x_prompt, x_sample
    p_hg, p_gdn, p_conv, s_hg_l, s_gdn_l, s_conv_l = [], [], [], [], [], []
    for l in range(DEPTH):
        lb = lower_bounds[l]
        z_hg = jnp.zeros((BATCH, HG_HEADS, HG_KEY, HG_VAL), f32)
        z_gdn = jnp.zeros((BATCH, GDN_HEADS, GDN_DK, GDN_DV), f32)
        z_conv = jnp.zeros((BATCH, CONV_W - 1, CONV_CH), f32)
        dp, a, b, c = hybrid_mixer(rmsnorm(hp, norm_w[l]), w_in[l], conv_w[l], lb,
                                   gdn_a_log[l], gdn_dt_bias[l], hg_out_norm[l],
                                   gdn_out_norm[l], w_out[l], z_hg, z_gdn, z_conv, True)
        hp = hp + dp
        p_hg.append(a.astype(x_prompt.dtype))
        p_gdn.append(b.astype(x_prompt.dtype))
        p_conv.append(c.astype(x_prompt.dtype))
        ds, a, b, c = hybrid_mixer(rmsnorm(hs, norm_w[l]), w_in[l], conv_w[l], lb,
                                   gdn_a_log[l], gdn_dt_bias[l], hg_out_norm[l],
                                   gdn_out_norm[l], w_out[l], state_hgrn[l], state_gdn[l],
                                   state_gdn_conv[l], False)
        hs = hs + ds
        s_hg_l.append(a.astype(state_hgrn.dtype))
        s_gdn_l.append(b.astype(state_gdn.dtype))
        s_conv_l.append(c.astype(state_gdn_conv.dtype))
    y_prompt = rmsnorm(hp, final_norm)
    y_sample = rmsnorm(hs, final_norm)
    new_hgrn_prompt = jnp.stack(p_hg, axis=0)
    new_gdn_prompt = jnp.stack(p_gdn, axis=0)
    new_conv_prompt = jnp.stack(p_conv, axis=0)
    new_hgrn_sample = jnp.stack(s_hg_l, axis=0)
    new_gdn_sample = jnp.stack(s_gdn_l, axis=0)
    new_conv_sample = jnp.stack(s_conv_l, axis=0)
    return (y_prompt, y_sample, new_hgrn_prompt, new_gdn_prompt, new_conv_prompt,
            new_hgrn_sample, new_gdn_sample, new_conv_sample)
```

```python
import contextlib
import numpy as np
import concourse.bass as bass
import concourse.mybir as mybir
from concourse.bass_utils import run_bass_kernel_spmd

F32 = mybir.dt.float32
BF16 = mybir.dt.bfloat16
ALU = mybir.AluOpType
AF = mybir.ActivationFunctionType
AX = mybir.AxisListType

ENGS = ("pe", "act", "dve", "pool", "sp")
PSUM_PREFIXES = ("acc", "trp", "rc", "pp")
T = 2048
D = 1024
DIN = 4104
NCORE = 8
NS = 16
TB = 512
NBLK = T // TB
EPS = 1e-6
NEG = -30000.0


class Sched:
    def __init__(self, nc):
        self.nc = nc
        self.ops = {e: [] for e in ENGS}
        self.cnt = {e: 0 for e in ENGS}
        self.res = {}
        self.seen = {e: {} for e in ENGS}
        self.dma_cnt = {}
        self.sem_names = set(ENGS)
        self.unwritten = set()

    def _need(self, eng, ev, waits):
        if ev is None:
            return
        s, v = ev
        if s == "pe" and eng == "pe":
            return
        if s in self.dma_cnt and v != self.dma_cnt[s]:
            raise RuntimeError("DMA sem %s: wait for %d but %d issued" % (s, v, self.dma_cnt[s]))
        if self.seen[eng].get(s, 0) >= v:
            return
        waits[s] = max(waits.get(s, 0), v)

    def _deps(self, eng, reads, writes):
        waits = {}
        for k in reads:
            r = self.res.get(k)
            if r:
                self._need(eng, r["w"], waits)
            elif not k.startswith(PSUM_PREFIXES):
                self.unwritten.add(k)
        for k in writes:
            r = self.res.get(k)
            if r:
                self._need(eng, r["w"], waits)
                for ev in r["r"]:
                    self._need(eng, ev, waits)
        for s, v in waits.items():
            self.seen[eng][s] = v
        return waits

    def _commit(self, ev, reads, writes):
        for k in reads:
            self.res.setdefault(k, {"w": None, "r": []})["r"].append(ev)
        for k in writes:
            self.res[k] = {"w": ev, "r": []}

    def op(self, eng, meth, *args, reads=(), writes=(), inc=True, after=None, **kw):
        fn = (meth, args, kw)
        writes = list(writes) + [k for k in reads if k.startswith(PSUM_PREFIXES)]
        waits = self._deps(eng, reads, writes)
        if after is not None:
            waits[after[0]] = max(waits.get(after[0], 0), after[1])
        if inc:
            self.cnt[eng] += 1
            ev = (eng, self.cnt[eng])
        else:
            ev = (eng, self.cnt[eng] + 1)
        self.ops[eng].append((waits, fn, (eng, 1) if inc else None))
        self._commit(ev, reads, writes)
        return ev

    def dma(self, eng, sem, out, in_, reads=(), writes=(), **kw):
        fn = ("dma_start", (), dict(out=out, in_=in_, **kw))
        self.sem_names.add(sem)
        self.dma_cnt.setdefault(sem, 0)
        waits = self._deps(eng, reads, writes)
        self.dma_cnt[sem] += 16
        ev = (sem, self.dma_cnt[sem])
        self.ops[eng].append((waits, fn, (sem, 16)))
        self._commit(ev, reads, writes)
        return ev

    def wait_all(self, eng, keys):
        waits = {}
        for k in keys:
            r = self.res.get(k)
            if r:
                self._need(eng, r["w"], waits)
                for ev in r["r"]:
                    self._need(eng, ev, waits)
        self.ops[eng].append((waits, None, None))

    def emit(self):
        nc = self.nc
        names = sorted(self.sem_names)
        with contextlib.ExitStack() as st:
            sems = {n: st.enter_context(nc.semaphore("s_" + n)) for n in names}
            block = st.enter_context(nc.Block())

            def run(engname):
                def body(e):
                    for waits, fn, inc in self.ops[engname]:
                        for s, v in waits.items():
                            e.wait_ge(sems[s], v)
                        if fn is None:
                            continue
                        ins = getattr(e, fn[0])(*fn[1], **fn[2])
                        if inc is not None:
                            ins.then_inc(sems[inc[0]], inc[1])
                return body

            block.tensor(run("pe"))
            block.scalar(run("act"))
            block.vector(run("dve"))
            block.gpsimd(run("pool"))
            block.sync(run("sp"))


C_HQ, C_HF, C_HI, C_HZ, C_GQ, C_GK, C_GV, C_GZ, C_GB = 0, 512, 1024, 1536, 2048, 2560, 3072, 3584, 4096


def build(dbg=None):
    dbg = dbg or set()
    nc = bass.Bass("TRN2", target_bir_lowering=False)

    def din(name, shape):
        return nc.dram_tensor(name, shape, F32, kind="ExternalInput").ap()

    def dout(name, shape):
        return nc.dram_tensor(name, shape, F32, kind="ExternalOutput").ap()

    xp = din("xp", [T, D]); xs = din("xs", [NS, D])
    shg = din("shg", [NS, 4, 128, 128]); sgd = din("sgd", [NS, 4, 128, 128]); scv = din("scv", [NS, 3, 1536])
    norm_w = din("norm_w", [D]); w_in = din("w_in", [D, DIN]); lbl = din("lbl", [2, 512])
    conv_w = din("conv_w", [4, 1536]); a_log = din("a_log", [4]); dt_bias = din("dt_bias", [4])
    hgn = din("hgn", [128]); gdnn = din("gdnn", [128]); w_out = din("w_out", [D, D]); fin = din("fin", [D])
    yp = dout("yp", [T, D]); ys = dout("ys", [NS, D])
    o_shg = dout("o_shg", [4, 128, 128]); o_sgd = dout("o_sgd", [4, 128, 128]); o_cv = dout("o_cv", [3, 1536])
    os_hg = dout("os_hg", [NS, 4, 128, 128]); os_gd = dout("os_gd", [NS, 4, 128, 128]); os_cv = dout("os_cv", [NS, 3, 1536])
    dbg_out = {}
    if "mix" in dbg:
        dbg_out["d_o"] = dout("d_o", [T, D])

    with contextlib.ExitStack() as st:
        def sb(name, shape, dt=F32):
            return st.enter_context(nc.sbuf_tensor(name, shape, dt))

        def ps(name, shape, dt=F32):
            return st.enter_context(nc.psum_tensor(name, shape, dt))

        S = Sched(nc)
        outs_done = []

        acc = [ps("acc%d" % i, [128, 512]) for i in range(2)]
        trp = [ps("trp%d" % i, [128, 1024], BF16) for i in range(2)]
        rcA = ps("rcA", [128, 512]); rcB = ps("rcB", [128, 512])
        pp = [ps("pp%d" % i, [128, 512]) for i in range(2)]
        mx = [rcA, rcB]
        misc = rcA
        rr = {"acc": 0, "trp": 0, "mx": 0, "pp": 0}
        PSN = {"acc": ["acc0", "acc1"], "trp": ["trp0", "trp1"], "mx": ["rcA", "rcB"], "pp": ["pp0", "pp1"]}

        def next_ps(kind):
            lst = {"acc": acc, "trp": trp, "mx": mx, "pp": pp}[kind]
            i = rr[kind] % len(lst)
            rr[kind] += 1
            return lst[i], PSN[kind][i]

        def act(out, in_, func, r, w, **kw):
            S.op("act", "activation", out, in_, func, reads=r, writes=w, **kw)

        def tt(eng, out, in0, in1, op, r, w):
            S.op(eng, "tensor_tensor", out, in0, in1, op, reads=r, writes=w)

        def ts(eng, out, in0, s1, s2, op0, op1, r, w):
            if s2 is None:
                S.op(eng, "tensor_scalar", out, in0, s1, None, op0, reads=r, writes=w)
            else:
                S.op(eng, "tensor_scalar", out, in0, s1, s2, op0, op1, reads=r, writes=w)

        def stt(out, in0, scalar, in1, op0, op1, r, w):
            S.op("dve", "scalar_tensor_tensor", out, in0, scalar, in1, op0, op1, reads=r, writes=w)

        def cp(eng, out, in_, r, w):
            if eng == "act":
                S.op("act", "activation", out, in_, AF.Copy, reads=r, writes=w)
            else:
                S.op(eng, "tensor_copy", out, in_, reads=r, writes=w)

        def mm(out, lhsT, rhs, start, stop, r, w, inc=True, after=None):
            return S.op("pe", "matmul", out, lhsT=lhsT, rhs=rhs, start=start, stop=stop, reads=r, writes=w, inc=inc, after=after)

        def tr(out, in_, ident, r, w, inc=True):
            S.op("pe", "transpose", out, in_, ident, reads=r, writes=w, inc=inc)

        def bc(ap, shape):
            return ap.to_broadcast(shape)

        dumped = set()

        def dump(name, ap, shape, res):
            if "dump" not in dbg or name in dumped:
                return
            dumped.add(name)
            d = dout("dd_" + name, shape)
            S.dma("sp", "dd_" + name, d, ap, reads=res)
            S.wait_all("sp", res)

        ident_bf = sb("ident_bf", [128, 128], BF16)
        ident_f = sb("ident_f", [128, 128])
        ones_bf = sb("ones_bf", [128, 128], BF16)
        ones_f = sb("ones_f", [128, 128])
        eps_c = sb("eps_c", [128, 1])
        hm = sb("hm", [128, 2])
        lnk_c = sb("lnk_c", [128, 1])
        fin_bc = sb("fin_bc", [128, D])
        normw_col = sb("normw_col", [128, 8])
        W1 = sb("W1", [128, 8, DIN], BF16)
        W2 = sb("W2", [128, 8, D], BF16)
        wn_col = sb("wn_col", [128, 2])
        lbl_sb = sb("lbl_sb", [128, 2, 4])
        lb_c = sb("lb_c", [128, 4]); lnoml_c = sb("lnoml_c", [128, 4]); lbt = sb("lbt", [128, 4])
        cw_c = sb("cw_c", [128, 12, 4])
        alog_bc = sb("alog_bc", [128, 4]); negA_bc = sb("negA_bc", [128, 4]); dtb_bc = sb("dtb_bc", [128, 4])
        BD = sb("BD", [128, 128]); Tri_bd = sb("Tri_bd", [128, 128]); LE = Tri_bd; SU_bd = sb("SU_bd", [128, 128])
        TriLoc = sb("TriLoc", [128, 64]); SUloc = sb("SUloc", [128, 64]); Iloc = sb("Iloc", [128, 64])
        NEGs4 = sb("NEGs4", [128, 4, 64]); NEGTi4 = sb("NEGTi4", [128, 4, 64])
        cmask = sb("cmask", [128, TB], BF16)
        blkscr = sb("blkscr", [128, 5 * TB])
        e_sb = blkscr[:, 0:TB]; l1 = blkscr[:, TB:2 * TB]; l2 = blkscr[:, 2 * TB:3 * TB]; bcum = blkscr[:, 3 * TB:4 * TB]; dd = blkscr[:, 4 * TB:5 * TB]
        xt2 = blkscr[:, 0:2 * TB]

        P = "pool"
        S.op(P, "memset", ident_bf[:], 1.0, writes=["ident_bf"])
        S.op(P, "affine_select", ident_bf[:], ident_bf[:], pattern=[[-1, 128]], compare_op=ALU.is_equal, fill=0.0, base=0,
             channel_multiplier=1, reads=["ident_bf"], writes=["ident_bf"])
        S.op(P, "memset", ident_f[:], 1.0, writes=["ident_f"])
        S.op(P, "affine_select", ident_f[:], ident_f[:], pattern=[[-1, 128]], compare_op=ALU.is_equal, fill=0.0, base=0,
             channel_multiplier=1, reads=["ident_f"], writes=["ident_f"])
        S.op(P, "memset", ones_bf[:], 1.0, writes=["ones_bf"])
        S.op(P, "memset", ones_f[:], 1.0, writes=["ones_f"])
        S.op(P, "memset", eps_c[:], EPS, writes=["eps_c"])
        S.op(P, "memset", lnk_c[:], -0.5 * float(np.log(128.0)), writes=["lnk_c"])
        S.op(P, "memset", hm[:], 0.0, writes=["hm"])
        S.op(P, "memset", hm[0:64, 0:1], 1.0, reads=["hm"], writes=["hm"])
        S.op(P, "memset", hm[64:128, 1:2], 1.0, reads=["hm"], writes=["hm"])
        S.op(P, "memset", LE[:], 1.0, writes=["Tri_bd"])
        S.op(P, "affine_select", LE[:], LE[:], pattern=[[1, 128]], compare_op=ALU.is_ge, fill=0.0, base=0,
             channel_multiplier=-1, reads=["Tri_bd"], writes=["Tri_bd"])
        S.op(P, "memset", BD[:], 0.0, writes=["BD"])
        S.op(P, "memset", BD[0:64, 0:64], 1.0, reads=["BD"], writes=["BD"])
        S.op(P, "memset", BD[64:128, 64:128], 1.0, reads=["BD"], writes=["BD"])
        tt(P, Tri_bd[:], LE[:], BD[:], ALU.mult, ["Tri_bd", "BD"], ["Tri_bd"])
        tt(P, SU_bd[:], BD[:], Tri_bd[:], ALU.subtract, ["BD", "Tri_bd"], ["SU_bd"])
        tt(P, TriLoc[:], Tri_bd[:, 0:64], Tri_bd[:, 64:128], ALU.add, ["Tri_bd"], ["TriLoc"])
        tt(P, SUloc[:], SU_bd[:, 0:64], SU_bd[:, 64:128], ALU.add, ["SU_bd"], ["SUloc"])
        tt(P, Iloc[:], ident_f[:, 0:64], ident_f[:, 64:128], ALU.add, ["ident_f"], ["Iloc"])
        for h in range(4):
            ts(P, NEGs4[:, h, :], SUloc[:], -NEG, NEG, ALU.mult, ALU.add, ["SUloc"], ["NEGs4"])
            ts(P, NEGTi4[:, h, :], TriLoc[:], -NEG, NEG, ALU.mult, ALU.add, ["TriLoc"], ["NEGTi4"])
        S.op(P, "memset", cmask[:], 1.0, writes=["cmask"])
        S.op(P, "memset", cmask[:].rearrange("p (c t) -> p c t", t=64)[:, :, 0:1], 0.0, reads=["cmask"], writes=["cmask"])

        S.dma("sp", "c_rows0", blkscr[0:4, 0:1536], conv_w, writes=["e_sb", "l1", "l2"])
        S.dma("sp", "c_rows1", blkscr[0:2, 1536:2048], lbl, writes=["bcum"])
        S.dma("sp", "c_rows2", blkscr[0:8, 2048:2176], norm_w.rearrange("(j p) -> j p", p=128), writes=["dd"])
        for ct_ in range(12):
            tr(pp[0][:, ct_ * 4:(ct_ + 1) * 4], blkscr[0:4, ct_ * 128:(ct_ + 1) * 128], ident_f[0:4, 0:4], ["e_sb", "l1", "l2", "ident_f"], ["pp0"], inc=False)
        for h_ in range(4):
            tr(pp[0][:, 48 + h_ * 2:50 + h_ * 2], blkscr[0:2, 1536 + h_ * 128:1536 + (h_ + 1) * 128], ident_f[0:2, 0:2], ["bcum", "ident_f"], ["pp0"], inc=False)
        tr(pp[0][:, 56:64], blkscr[0:8, 2048:2176], ident_f[0:8, 0:8], ["dd", "ident_f"], ["pp0"])
        cp("dve", normw_col[:], pp[0][:, 56:64], ["pp0"], ["normw_col"])
        cp("dve", cw_c[:], pp[0][:, 0:48].rearrange("p (t j) -> p t j", j=4), ["pp0"], ["cw_c"])
        cp("dve", lbl_sb[:].rearrange("p r h -> p h r"), pp[0][:, 48:56].rearrange("p (h r) -> p h r", r=2), ["pp0"], ["lbl_sb"])
        S.dma("sp", "c_wn0", wn_col[:, 0:1], hgn.rearrange("(p o) -> p o", o=1), writes=["wn0"])
        S.dma("sp", "c_wn1", wn_col[:, 1:2], gdnn.rearrange("(p o) -> p o", o=1), writes=["wn1"])
        o_sb = [sb("o_sb%d" % i, [128, D]) for i in range(2)]
        xr = sb("xt0", [128, D]); otmp = sb("otmp", [128, D])

        w_in_v = w_in.rearrange("(j p) c -> p j c", p=128)
        wq = []

        def mk_w1(n_, c0, jp):
            def f():
                stg = [(o_sb[0], "o_sb0"), (o_sb[1], "o_sb1"), (otmp, "otmp")]
                buf, bn = stg[n_ % 3]
                bv = buf[:].rearrange("p (j c) -> p j c", j=2)
                S.dma("sp", "stg_" + bn, bv, w_in_v[:, 2 * jp:2 * jp + 2, c0:c0 + 512], writes=[bn])
                wres = ["W1.%d.%d" % (c0 // 128 + q, jp) for q in range(4)]
                if n_ % 2 == 0:
                    tt("dve", W1[:, 2 * jp:2 * jp + 2, c0:c0 + 512], bv, bc(normw_col[:, 2 * jp:2 * jp + 2].unsqueeze(2), [128, 2, 512]), ALU.mult,
                       [bn, "normw_col"], wres)
                else:
                    for q_ in range(2):
                        act(W1[:, 2 * jp + q_, c0:c0 + 512], bv[:, q_, :], AF.Copy, [bn, "normw_col"], wres if q_ == 1 else [],
                            scale=normw_col[:, 2 * jp + q_:2 * jp + q_ + 1])
            return f

        def w_bg():
            bv8 = otmp[:, 0:64].rearrange("p (j c) -> p j c", j=8)
            S.dma("sp", "stg_otmp", bv8, w_in_v[:, :, C_GB:C_GB + 8], writes=["otmp"])
            tt("dve", W1[:, :, C_GB:C_GB + 8], bv8, bc(normw_col[:].unsqueeze(2), [128, 8, 8]), ALU.mult, ["otmp", "normw_col"],
               ["W1.32.%d" % jp for jp in range(4)])

        def mk_w2(j):
            def f():
                stg2 = [(otmp, "otmp"), (o_sb[0], "o_sb0"), (o_sb[1], "o_sb1")]
                buf, bn = stg2[j % 3]
                S.dma("sp", "stg_" + bn, buf[:], w_out[j * 128:(j + 1) * 128, :], writes=[bn])
                g = 0 if j < 4 else 1
                act(W2[:, j, :], buf[:], AF.Copy, [bn, "wn%d" % g], ["W2.%d" % j], scale=wn_col[:, g:g + 1])
            return f
        n_ = 0
        for c0 in [C_GQ, C_GK, C_GV, C_HZ, C_GZ, C_HF, C_HQ, C_HI]:
            for jp in range(4):
                wq.append(mk_w1(n_, c0, jp))
                n_ += 1
        wq.append(w_bg)
        for j in range(8):
            wq.append(mk_w2(j))

        wdone = [0]

        def emit_weights(k):
            for _ in range(min(k, len(wq))):
                wq.pop(0)()
                wdone[0] += 1

        def ensure_weights(n):
            while wdone[0] < n and wq:
                emit_weights(1)

        def w1res(c0, c1):
            return ["W1.%d.%d" % (cb, jp) for cb in range(c0 // 128, (c1 - 1) // 128 + 1) for jp in range(4)]

        S.dma("act", "c_fin", fin_bc[:], fin.partition_broadcast(128), writes=["fin_bc"])
        S.dma("act", "c_alog", alog_bc[:], a_log.partition_broadcast(128), writes=["alog_bc"])
        S.dma("act", "c_dtb", dtb_bc[:], dt_bias.partition_broadcast(128), writes=["dtb_bc"])

        xt = [xr] * 2
        ssq = [sb("ssq%d" % i, [128, 1]) for i in range(2)]
        hbf = [sb("hbf0", [128, D], BF16)] * 2
        hTb = sb("hT", [128, 8, TB], BF16)
        v_tok = sb("v_tok", [128, 4, 512], BF16)
        bg_sb = sb("bg_sb", [128, 4, 8])
        qTt = sb("qTt", [128, 4, TB], BF16); kTt = sb("kTt", [128, 4, TB], BF16)
        ktok = sb("ktok", [128, 4, 512], BF16)
        sTm = sb("sTm", [128, 4, 4, 64], BF16)
        ebl = sb("ebl", [128, 4, 8])
        S_hg = sb("S_hg", [128, 4, 128]); Sp_bf = sb("Sp_bf", [128, 4, 128], BF16)
        S_gd = sb("S_gd", [128, 4, 128]); Sg_bf = sb("Sg_bf", [128, 4, 128], BF16)
        pre = [sb("pre%d" % i, [128, TB + 3]) for i in range(2)]
        cacc = [sb("cacc%d" % i, [128, TB]) for i in range(2)]
        chist = sb("chist", [128, 12, 3])
        kqT = sb("kqT", [128, 4, 2, TB], BF16); kcT = kqT[:, :, 0, :]; qcT = kqT[:, :, 1, :]; vcT = sb("vcT", [128, 4, TB], BF16)
        g_t = sb("g_t", [128, 4, 4]); beta_t = sb("beta_t", [128, 4, 4]); st1 = sb("st1", [128, 4, 4]); st2 = sb("st2", [128, 4, 4])
        lnr = sb("lnr", [128, 4, 8]); rk_t = sb("rk_t", [128, 4, 4])
        GG = sb("GG", [128, 4, 8]); gm = sb("gm", [128, 4, 2, 4]); eGl = sb("eGl", [128, 4, 2, 4])
        f_rhsk = sb("f_rhsk", [128, 4, 4]); f_dec = sb("f_dec", [128, 4, 4]); nbr = sb("nbr", [128, 4, 4]); f_o = sb("f_o", [128, 4, 4])
        rhs_vk = [sb("rhs_vk%d" % i, [128, 4, 256], BF16) for i in range(2)]
        kdec = [sb("kdec%d" % i, [128, 512], BF16) for i in range(2)]
        R1 = sb("R1", [128, 4, 64]); R2 = sb("R2", [128, 4, 64]); R3 = sb("R3", [128, 4, 64]); R4 = sb("R4", [128, 4, 64])
        LL = sb("LL", [128, 512]); tmpA = sb("tmpA", [128, 4, 64]); tmpB = sb("tmpB", [128, 4, 64])
        X0 = sb("X0", [128, 4, 64], BF16)
        attnT = [sb("attnT%d" % i, [128, 4, 64], BF16) for i in range(2)]
        XZ = [sb("XZ%d" % i, [128, 2, 4, 64], BF16) for i in range(2)]
        Qb = [sb("Qb%d" % i, [128, 4, 64], BF16) for i in range(2)]
        Qfin = [sb("Qfin%d" % i, [128, 4, 64], BF16) for i in range(2)]
        nWkT = [sb("nWkT%d" % i, [128, 4, 128], BF16) for i in range(2)]
        u_bf = sb("u_bf", [128, 4, 128], BF16)
        sqb = [u_bf[:].rearrange("p h v -> p (h v)")] * 2
        sz = sb("sz", [128, 4, D], BF16)
        ssq8 = sb("ssq8", [128, 8])
        ybf = sb("ybf", [128, D], BF16); yT = sb("yT", [128, 8, 128], BF16)
        ssqf = sb("ssqf", [128, 1])

        H4 = lambda n: ["%s.%d" % (n, h_) for h_ in range(4)]
        S.op(P, "memset", S_hg[:], 0.0, writes=["S_hg"] + H4("S_hg"))
        S.op(P, "memset", S_gd[:], 0.0, writes=["S_gd"] + H4("S_gd"))
        S.op(P, "memset", Sg_bf[:], 0.0, writes=["Sg_bf"] + H4("Sg_bf"))
        S.op(P, "memset", chist[:], 0.0, writes=["chist"])

        def rstd_from_ssq(col, rows, n, r):
            act(col, col, AF.Ln, [r, "eps_c"], [r], bias=eps_c[0:rows, :], scale=1.0 / n)
            act(col, col, AF.Exp, [r], [r], scale=-0.5)

        def fm_proj(c0, evac):
            pa, pan = next_ps("acc")
            for j in range(8):
                mm(pa[:, :], W1[:, j, c0:c0 + 128], hTb[:, j, :], j == 0, j == 7, ["hT"] + w1res(c0, c0 + 128), [pan], inc=(j == 7))
            evac(pa, pan)

        def tm_proj(i, c0, ncols, evac):
            pa, pan = next_ps("acc")
            for j in range(8):
                mm(pa[:, 0:ncols], hTb[:, j, i * 128:(i + 1) * 128], W1[:, j, c0:c0 + ncols], j == 0, j == 7,
                   ["hT"] + w1res(c0, c0 + ncols), [pan], inc=(j == 7))
            evac(pa, pan)

        def emit_derived_consts():
            tt("dve", lbt[:], lbl_sb[:, 1, :], lbl_sb[:, 0, :], ALU.subtract, ["lbl_sb"], ["lbt"])
            act(lb_c[:], lbt[:], AF.Exp, ["lbt"], ["lb_c"])
            act(lb_c[:], lb_c[:], AF.Ln, ["lb_c"], ["lb_c"], bias=1.0)
            act(lb_c[:], lb_c[:], AF.Exp, ["lb_c"], ["lb_c"], scale=-1.0)
            act(lnoml_c[:], lbt[:], AF.Exp, ["lbt"], ["lnoml_c"], scale=-1.0)
            act(lnoml_c[:], lnoml_c[:], AF.Ln, ["lnoml_c"], ["lnoml_c"], bias=1.0)
            ts("dve", lnoml_c[:], lnoml_c[:], -1.0, None, ALU.mult, None, ["lnoml_c"], ["lnoml_c"])
            act(negA_bc[:], alog_bc[:], AF.Exp, ["alog_bc"], ["negA_bc"])
            ts("dve", negA_bc[:], negA_bc[:], -1.0, None, ALU.mult, None, ["negA_bc"], ["negA_bc"])


        tcount = [0]
        pcount = [0]
        dbgq = []

        def phase_x(blk, i, xbuf, xres, xsem, nxt=None):
            tok0 = blk * TB + i * 128
            slot = tcount[0] % 2
            tcount[0] += 1
            sn = "ssq%d" % slot; hn = "hbf0"
            pre_ = blk >= 1
            xq = "sp" if pre_ else "act"
            if i == 0:
                S.dma(xq, xsem, xbuf[:], xp[tok0:tok0 + 128, :], writes=xres)
            if nxt is not None and i < 3:
                S.dma(xq, nxt[2], nxt[0][:], xp[tok0 + 128:tok0 + 256, :], writes=nxt[1])
            act(hbf[0][:], xbuf[:], AF.Square, xres, [hn, sn], accum_out=ssq[slot][:])
            rstd_from_ssq(ssq[slot][:], 128, D, sn)
            ts("dve", hbf[slot][:], xbuf[:], ssq[slot][:], None, ALU.mult, None, xres + [sn], [hn])
            if i < 3 and nxt is None:
                S.dma(xq, xsem, xbuf[:], xp[tok0 + 128:tok0 + 256, :], writes=xres)
            pt, ptn = next_ps("trp")
            for j in range(8):
                tr(pt[:, j * 128:(j + 1) * 128], hbf[slot][:, j * 128:(j + 1) * 128], ident_bf[:], [hn, "ident_bf"], [ptn], inc=(j == 7))
            cp("act" if i % 2 == 0 else "dve", hTb[:, :, i * 128:(i + 1) * 128], pt[:].rearrange("p (j t) -> p j t", j=8), [ptn], ["hT"])


        for blk in range(NBLK):
            if blk == 0:
                for i in range(4):
                    xb_ = [(xr, ["xr"], "xr"), (xt2, ["e_sb", "l1"], "xt2")]
                    phase_x(0, i, xb_[i % 2][0], xb_[i % 2][1], xb_[i % 2][2], nxt=xb_[(i + 1) % 2])
                    emit_weights(3)
            deferred = []
            deferred2 = []
            for kind, cbase, dst, dn in ((0, C_GQ, qcT, "qcT"), (1, C_GK, kcT, "kcT"), (2, C_GV, vcT, "vcT")):
                for h in range(4):
                    ct = kind * 4 + h
                    if blk == 0:
                        ensure_weights(4 * (kind + 1))
                        emit_weights(3)
                    pslot = pcount[0] % 2
                    pcount[0] += 1
                    pb = pre[pslot]; pn = "pre%d" % pslot
                    ca = cacc[pcount[0] % 2]; cn = "cacc%d" % (pcount[0] % 2)

                    def ev_c(pa, pan, pb=pb, pn=pn, ca=ca, cn=cn, ct=ct):
                        cp("act", pb[:, 3:TB + 3], pa[:, :], [pan], [pn])
                        act(ca[:], pa[:, :], AF.Copy, [pan, "cw_c"], [cn], scale=cw_c[:, ct, 3:4])
                    cp(P, pb[:, 0:3], chist[:, ct, :], ["chist", "chist.%d" % ct], [pn])
                    fm_proj(cbase + h * 128, ev_c)
                    stt(ca[:], pb[:, 2:TB + 2], cw_c[:, ct, 2:3], ca[:], ALU.mult, ALU.add, [pn, cn, "cw_c"], [cn])
                    stt(ca[:], pb[:, 1:TB + 1], cw_c[:, ct, 1:2], ca[:], ALU.mult, ALU.add, [pn, cn], [cn])
                    stt(ca[:], pb[:, 0:TB], cw_c[:, ct, 0:1], ca[:], ALU.mult, ALU.add, [pn, cn], [cn])
                    cp(P, chist[:, ct, :], pb[:, TB:TB + 3], [pn], ["chist.%d" % ct])
                    if blk == NBLK - 1:
                        S.dma("sp", "o_cv%d" % pslot, o_cv[:, ct * 128:(ct + 1) * 128].rearrange("j c -> c j"), pb[:, TB:TB + 3], reads=[pn],
                              allow_slow_non_contiguous=True)
                        outs_done.append(pn)
                    def post(kind=kind, h=h, dst=dst, dn=dn, ca=ca, cn=cn):
                        act(dst[:, h, :], ca[:], AF.Silu, [cn], ["%s.%d" % (dn, h)])
                        if kind < 2:
                            sq = sqb[0]; sqn = "u_bf"
                            tt(P, sq, dst[:, h, :], dst[:, h, :], ALU.mult, ["%s.%d" % (dn, h)], [sqn])

                            def post2():
                                for i in range(4):
                                    mm(misc[:, 64 + i * 8 + kind * 4 + h: 64 + i * 8 + kind * 4 + h + 1], sq[:, i * 128:(i + 1) * 128], ones_bf[:, 0:1], True, True,
                                       [sqn, "ones_bf"], ["rcA"], inc=(i == 3))
                            deferred2.append(post2)
                    while len(deferred2) > 0:
                        deferred2.pop(0)()
                    deferred.append(post)
                    while len(deferred) > 1:
                        deferred.pop(0)()
            while deferred:
                deferred.pop(0)()
            while deferred2:
                deferred2.pop(0)()
            if blk == 0:
                emit_weights(1000)
            for i in range(4):
                for half, cbase in ((0, C_HZ), (1, C_GZ)):
                    def ev_z(pa, pan, i=i, half=half):
                        act(sz[:, i, half * 512:(half + 1) * 512], pa[:, :], AF.Silu, [pan], ["sz.%d.%d" % (i, half)])
                    tm_proj(i, cbase, 512, ev_z)

            if blk == 0:
                emit_derived_consts()
            setA = dict(e=e_sb, l1=l1, l2=l2, bc=bcum, dd=dd, n=dict(e="e_sb", l1="l1", l2="l2", bc="bcum", dd="dd"))
            setB = dict(e=o_sb[0][:, 0:TB], l1=o_sb[0][:, TB:2 * TB], l2=o_sb[1][:, 0:TB], bc=o_sb[1][:, TB:2 * TB], dd=otmp[:, 0:TB],
                        n=dict(e="o_sb0", l1="o_sb0", l2="o_sb1", bc="o_sb1", dd="otmp"))

            def chain_ops(h, st_):
                e_, l1_, l2_, bc_, dd_ = st_["e"], st_["l1"], st_["l2"], st_["bc"], st_["dd"]
                n = st_["n"]
                b3 = bc_.rearrange("p (c t) -> p c t", t=64)
                ops = []

                def ev_hf(pa, pan):
                    act(e_, pa[:, :], AF.Exp, [pan], [n["e"]], scale=-1.0)
                ops.append(lambda: fm_proj(C_HF + h * 128, ev_hf))
                ops.append(lambda: act(l1_, e_, AF.Ln, [n["e"], "lb_c"], [n["l1"]], scale=lb_c[:, h:h + 1], bias=1.0))
                ops.append(lambda: act(l2_, e_, AF.Ln, [n["e"]], [n["l2"]], bias=1.0))
                ops.append(lambda: tt(P, l1_, l1_, l2_, ALU.subtract, [n["l1"], n["l2"]], [n["l1"]]))
                ops.append(lambda: S.op("dve", "tensor_tensor_scan", bc_, cmask[:], l1_, 0.0, ALU.mult, ALU.add, reads=["cmask", n["l1"]], writes=[n["bc"]]))
                ops.append(lambda: tt(P, dd_.rearrange("p (c t) -> p c t", t=64), b3, bc(b3[:, :, 63:64], [128, 8, 64]), ALU.subtract, [n["bc"]], [n["dd"]]))
                ops.append(lambda: act(ebl[:, h, :], b3[:, :, 63], AF.Exp, [n["bc"]], ["ebl.%d" % h]))
                ops.append(lambda: tt(P, l2_, l2_, dd_, ALU.add, [n["l2"], n["dd"]], [n["l2"]]))
                ops.append(lambda: act(dd_, dd_, AF.Exp, [n["dd"]], [n["dd"]]))
                ops.append(lambda: act(l2_, l2_, AF.Exp, [n["l2"], "lnoml_c"], [n["l2"]], scale=-1.0, bias=lnoml_c[:, h:h + 1]))
                ops.append(lambda: tt("dve", kTt[:, h, :], e_, l2_, ALU.mult, [n["e"], n["l2"]], ["kTt.%d" % h]))

                def ev_hq(pa, pan):
                    tt("dve", qTt[:, h, :], pa[:, :], dd_, ALU.mult, [pan, n["dd"]], ["qTt.%d" % h])
                ops.append(lambda: fm_proj(C_HQ + h * 128, ev_hq))
                return ops

            def tm_ops(i):
                def ev_v(pa, pan):
                    cp("dve", v_tok[:, i, :], pa[:, :], [pan], ["v_tok.%d" % i])

                def ev_bg(pa, pan):
                    cp("act", bg_sb[:, i, :], pa[:, 0:8], [pan], ["bg_sb"])
                return [lambda: tm_proj(i, C_HI, 512, ev_v), lambda: tm_proj(i, C_GB, 8, ev_bg)]

            for pair in range(2):
                ca_, cb_ = chain_ops(2 * pair, setA), chain_ops(2 * pair + 1, setB)
                extra = tm_ops(2 * pair) + tm_ops(2 * pair + 1)
                for k_ in range(len(ca_)):
                    ca_[k_]()
                    cb_[k_]()
                    if k_ in (3, 5, 7, 9) and extra:
                        extra.pop(0)()
                while extra:
                    extra.pop(0)()

            tt(P, st1[:], bg_sb[:, :, 4:8], bc(dtb_bc[:].unsqueeze(1), [128, 4, 4]), ALU.add, ["bg_sb", "dtb_bc"], ["st1"])
            act(st1[:], st1[:], AF.Exp, ["st1"], ["st1"])
            act(st1[:], st1[:], AF.Ln, ["st1"], ["st1"], bias=1.0)
            tt(P, g_t[:], st1[:], bc(negA_bc[:].unsqueeze(1), [128, 4, 4]), ALU.mult, ["st1", "negA_bc"], ["g_t"])
            act(st2[:], bg_sb[:, :, 0:4], AF.Exp, ["bg_sb"], ["st2"], scale=-1.0)
            act(st2[:], st2[:], AF.Ln, ["st2"], ["st2"], bias=1.0)
            act(beta_t[:], st2[:], AF.Exp, ["st2"], ["beta_t"], scale=-1.0)
            act(lnr[:], misc[:, 64:96].rearrange("p (i e) -> p i e", e=8), AF.Ln, ["rcA", "eps_c"], ["lnr"], bias=eps_c[:, :])
            ts("dve", lnr[:], lnr[:], -0.5, None, ALU.mult, None, ["lnr"], ["lnr"])
            ts("dve", lnr[:, :, 0:4], lnr[:, :, 0:4], lnk_c[:, 0:1], None, ALU.add, None, ["lnr", "lnk_c"], ["lnr"])
            act(rk_t[:], lnr[:, :, 4:8], AF.Exp, ["lnr"], ["rk_t"])
            ts("dve", gm[:, :, 0, :], g_t[:], hm[:, 0:1], None, ALU.mult, None, ["g_t", "hm"], ["gm"])
            ts("dve", gm[:, :, 1, :], g_t[:], hm[:, 1:2], None, ALU.mult, None, ["g_t", "hm", "gm"], ["gm"])
            pm, pmn = misc, "rcA"
            for i in range(4):
                mm(pm[:, i * 8:i * 8 + 4], Tri_bd[:], g_t[:, i, :], True, True, ["Tri_bd", "g_t"], [pmn], inc=False)
                mm(pm[:, i * 8 + 4:i * 8 + 8], BD[:], g_t[:, i, :], True, True, ["BD", "g_t"], [pmn], inc=False)
            mm(pm[:, 32:64], ones_f[:], gm[:].rearrange("p i c h -> p (i c h)"), True, True, ["ones_f", "gm"], [pmn])
            cp("dve", GG[:], pm[:, 0:32].rearrange("p (i e) -> p i e", e=8), [pmn], ["GG"])
            act(eGl[:].rearrange("p i c h -> p (i c h)"), pm[:, 32:64], AF.Exp, [pmn], ["eGl"])
            tt(P, st1[:], GG[:, :, 0:4], lnr[:, :, 4:8], ALU.add, ["GG", "lnr"], ["st1"])
            act(f_rhsk[:], st1[:], AF.Exp, ["st1"], ["f_rhsk"])
            tt(P, f_rhsk[:], f_rhsk[:], beta_t[:], ALU.mult, ["f_rhsk", "beta_t"], ["f_rhsk"])
            tt(P, st2[:], GG[:, :, 4:8], st1[:], ALU.subtract, ["GG", "st1"], ["st2"])
            tt(P, st1[:], lnr[:, :, 4:8], lnr[:, :, 4:8], ALU.add, ["lnr", "st2"], ["st1"])
            tt(P, st2[:], st2[:], st1[:], ALU.add, ["st1", "st2"], ["st2"])
            act(f_dec[:], st2[:], AF.Exp, ["st2"], ["f_dec"])
            tt(P, nbr[:], beta_t[:], rk_t[:], ALU.mult, ["beta_t", "rk_t"], ["nbr"])
            ts("dve", nbr[:], nbr[:], -1.0, None, ALU.mult, None, ["nbr"], ["nbr"])
            tt(P, st1[:], GG[:, :, 0:4], lnr[:, :, 0:4], ALU.add, ["GG", "lnr", "f_rhsk"], ["st1"])
            act(f_o[:], st1[:], AF.Exp, ["st1"], ["f_o"])

            dump("GG", GG[:].rearrange("p i e -> p (i e)"), [128, 32], ["GG"])
            dump("eGl", eGl[:].rearrange("p i c h -> p (i c h)"), [128, 32], ["eGl"])
            dump("f_dec", f_dec[:].rearrange("p i e -> p (i e)"), [128, 16], ["f_dec"])
            dump("f_o", f_o[:].rearrange("p i e -> p (i e)"), [128, 16], ["f_o"])
            dump("f_rhsk", f_rhsk[:].rearrange("p i e -> p (i e)"), [128, 16], ["f_rhsk"])
            dump("g_t", g_t[:].rearrange("p i e -> p (i e)"), [128, 16], ["g_t"])
            dump("beta_t", beta_t[:].rearrange("p i e -> p (i e)"), [128, 16], ["beta_t"])
            dump("lnr", lnr[:].rearrange("p i e -> p (i e)"), [128, 32], ["lnr"])
            def tile_stages(i):
                tok0 = blk * TB + i * 128
                tsl = slice(i * 128, (i + 1) * 128)
                par = i % 2
                rv = rhs_vk[par]; rvn = "rhs_vk%d" % par
                kd = kdec[par]; kdn = "kdec%d" % par
                at = attnT[par]; atn = "attnT%d" % par
                nw = nWkT[par]; nwn = "nWkT%d" % par
                qf = Qfin[par]; qfn = "Qfin%d" % par
                osb = o_sb[par]; osn = "o_sb%d" % par
                prep, post = [], []

                def blk_mm(out_fn, l_fn, r_fn, reads, pn_, last):
                    for h in range(4):
                        for c2 in range(2):
                            pr = slice(c2 * 64, c2 * 64 + 64)
                            mm(out_fn(pr, h), l_fn(pr, h), r_fn(pr, h), True, True, reads, [pn_], inc=(last and h == 3 and c2 == 1))

                def p_hgrn():
                    pt, ptn = next_ps("trp")
                    for h in range(4):
                        tr(pt[:, h * 128:(h + 1) * 128], kTt[:, h, tsl], ident_bf[:], ["kTt.%d" % h, "ident_bf"], [ptn], inc=(h == 3))
                    cp("act", ktok[:, i, :], pt[:, 0:512], [ptn], ["ktok.%d" % i])
                    pm, pmn = next_ps("pp")
                    for h in range(4):
                        for c2 in range(2):
                            pr = slice(c2 * 64, c2 * 64 + 64)
                            cs = slice(i * 128 + c2 * 64, i * 128 + c2 * 64 + 64)
                            mm(pm[pr, h * 64:(h + 1) * 64], kTt[:, h, cs], qTt[:, h, cs], True, True, ["kTt.%d" % h, "qTt.%d" % h], [pmn],
                               inc=(h == 3 and c2 == 1))
                    tt("dve", sTm[:, i, :, :], pm[:, 0:256].rearrange("p (h t) -> p h t", h=4), bc(TriLoc[:].unsqueeze(1), [128, 4, 64]), ALU.mult,
                       [pmn, "TriLoc"], ["sTm.%d" % i])
                prep.append(p_hgrn)

                def p_tok():
                    pt, ptn = next_ps("trp")
                    for h in range(4):
                        tr(pt[:, h * 128:(h + 1) * 128], kcT[:, h, tsl], ident_bf[:], ["kcT.%d" % h, "ident_bf"], [ptn], inc=False)
                    for h in range(4):
                        tr(pt[:, 512 + h * 128:512 + (h + 1) * 128], vcT[:, h, tsl], ident_bf[:], ["vcT.%d" % h, "ident_bf"], [ptn], inc=(h == 3))
                    tt("dve", rv[:, :, 0:128], pt[:, 512:1024].rearrange("p (h v) -> p h v", h=4), bc(beta_t[:, i, :].unsqueeze(2), [128, 4, 128]), ALU.mult,
                       [ptn, "beta_t"], [rvn])
                    tt("dve", rv[:, :, 128:256], pt[:, 0:512].rearrange("p (h v) -> p h v", h=4), bc(f_rhsk[:, i, :].unsqueeze(2), [128, 4, 128]), ALU.mult,
                       [ptn, "f_rhsk", rvn], [rvn])
                    tt("dve", kd[:].rearrange("p (h v) -> p h v", h=4), pt[:, 0:512].rearrange("p (h v) -> p h v", h=4),
                       bc(f_dec[:, i, :].unsqueeze(2), [128, 4, 128]), ALU.mult, [ptn, "f_dec"], [kdn])
                prep.append(p_tok)

                def p_mats():
                    pk, pkn = next_ps("pp")
                    for h in range(4):
                        for c2 in range(2):
                            pr = slice(c2 * 64, c2 * 64 + 64)
                            cs = slice(i * 128 + c2 * 64, i * 128 + c2 * 64 + 64)
                            mm(pk[pr, h * 128:(h + 1) * 128], kcT[:, h, cs], kqT[:, h, :, cs], True, True, ["kcT.%d" % h, "qcT.%d" % h], [pkn],
                               inc=(h == 3 and c2 == 1))
                    pk4 = pk[:, :].rearrange("p (h a s) -> p h a s", h=4, a=2)
                    gi = bc(g_t[:, i, :].unsqueeze(2), [128, 4, 64])
                    tt(P, R1[:], bc(SUloc[:].unsqueeze(1), [128, 4, 64]), gi, ALU.mult, ["SUloc", "g_t"], ["R1"])
                    tt(P, R2[:], bc(Iloc[:].unsqueeze(1), [128, 4, 64]), bc(lnr[:, i, 4:8].unsqueeze(2), [128, 4, 64]), ALU.mult, ["Iloc", "lnr"], ["R2"])
                    tt(P, R3[:], bc(TriLoc[:].unsqueeze(1), [128, 4, 64]), gi, ALU.mult, ["TriLoc", "g_t"], ["R3"])
                    tt(P, R4[:], bc(Iloc[:].unsqueeze(1), [128, 4, 64]), bc(lnr[:, i, 0:4].unsqueeze(2), [128, 4, 64]), ALU.mult, ["Iloc", "lnr"], ["R4"])
                    pd, pdn = next_ps("pp")
                    fl = lambda t_: t_[:].rearrange("p h s -> p (h s)")
                    mm(pd[:, 0:256], ident_f[:], fl(NEGs4), True, False, ["ident_f", "NEGs4"], [pdn], inc=False)
                    mm(pd[:, 0:256], Tri_bd[:], fl(R1), False, False, ["Tri_bd", "R1"], [pdn], inc=False)
                    mm(pd[:, 0:256], BD[:], fl(R2), False, True, ["BD", "R2"], [pdn], inc=False)
                    mm(pd[:, 256:512], ident_f[:], fl(NEGTi4), True, False, ["ident_f", "NEGTi4"], [pdn], inc=False)
                    mm(pd[:, 256:512], SU_bd[:], fl(R3), False, False, ["SU_bd", "R3"], [pdn], inc=False)
                    mm(pd[:, 256:512], BD[:], fl(R4), False, True, ["BD", "R4"], [pdn], inc=True)
                    act(LL[:], pd[:, :], AF.Exp, [pdn], ["LL"])
                    tt("dve", tmpA[:], pk4[:, :, 0, :], LL[:, 0:256].rearrange("p (h s) -> p h s", h=4), ALU.mult,
                       [pkn, "LL"], ["tmpA"])
                    tt(P, X0[:], tmpA[:], bc(nbr[:, i, :].unsqueeze(2), [128, 4, 64]), ALU.mult, ["tmpA", "nbr"], ["X0"])
                    tt("dve", tmpB[:], pk4[:, :, 1, :], LL[:, 256:512].rearrange("p (h s) -> p h s", h=4), ALU.mult,
                       [pkn, "LL"], ["tmpB"])
                    tt(P, at[:], tmpB[:], bc(rk_t[:, i, :].unsqueeze(2), [128, 4, 64]), ALU.mult, ["tmpB", "rk_t"], [atn])
                prep.append(p_mats)

                st = {}

                def p_z0():
                    pz, pzn = next_ps("pp")
                    blk_mm(lambda pr, h: pz[pr, h * 64:(h + 1) * 64], lambda pr, h: X0[pr, h, :], lambda pr, h: ident_bf[pr, pr], ["X0", "ident_bf"], pzn, True)
                    xz = XZ[0]; xzn = "XZ0"
                    cp(P, xz[:, 0, :, :], X0[:], ["X0"], [xzn])
                    cp("act", xz[:, 1, :, :], pz[:, 0:256].rearrange("p (h s) -> p h s", h=4), [pzn, xzn], [xzn])
                    tt("dve", Qb[0][:], pz[:, 0:256].rearrange("p (h s) -> p h s", h=4), bc(Iloc[:].unsqueeze(1), [128, 4, 64]), ALU.add, [pzn, "Iloc"], ["Qb0"])
                    st["q"] = (Qb[0], "Qb0")
                prep.append(p_z0)

                def mk_level(lev):
                    def p_lev():
                        qcur, qn = st["q"]
                        xz = XZ[lev % 2]; xzn = "XZ%d" % (lev % 2)
                        xz2 = XZ[(lev + 1) % 2]; xz2n = "XZ%d" % ((lev + 1) % 2)
                        px, pxn = next_ps("pp")
                        blk_mm(lambda pr, h: px[pr, h * 64:(h + 1) * 64], lambda pr, h: xz[pr, 1, h, :], lambda pr, h: xz[pr, 0, h, :], [xzn], pxn, False)
                        blk_mm(lambda pr, h: px[pr, 256 + h * 64:256 + (h + 1) * 64], lambda pr, h: xz[pr, 0, h, :], lambda pr, h: xz[pr, 1, h, :], [xzn], pxn, True)
                        cp("act", xz2[:].rearrange("p a h s -> p (a h s)"), px[:, :], [pxn], [xz2n])
                        pq, pqn = next_ps("pp")
                        blk_mm(lambda pr, h: pq[pr, h * 64:(h + 1) * 64], lambda pr, h: xz2[pr, 0, h, :], lambda pr, h: qcur[pr, h, :], [xz2n, qn], pqn, True)
                        if lev < 4:
                            qnx, qnn = Qb[(lev + 1) % 2], "Qb%d" % ((lev + 1) % 2)
                        else:
                            qnx, qnn = qf, qfn
                        tt("dve", qnx[:], pq[:, 0:256].rearrange("p (h s) -> p h s", h=4), qcur[:], ALU.add, [pqn, qn], [qnn])
                        st["q"] = (qnx, qnn)
                    return p_lev
                for lev in range(5):
                    prep.append(mk_level(lev))

                def p_wk():
                    for c2 in range(2):
                        pr = slice(c2 * 64, c2 * 64 + 64)
                        pw, pwn = next_ps("pp")
                        for h in range(4):
                            mm(pw[:, h * 64:(h + 1) * 64], rv[pr, h, 128:256], qf[pr, h, :], True, True, [rvn, qfn], [pwn], inc=(h == 3))
                        act(nw[:, :, c2 * 64:(c2 + 1) * 64], pw[:, 0:256].rearrange("p (h t) -> p h t", h=4), AF.Copy, [pwn, nwn], [nwn], scale=-1.0)
                prep.append(p_wk)

                def mk_hg(c2):
                    def r_hg():
                        po, pon = rcA, "rcA"
                        c = i * 2 + c2
                        pr = slice(c2 * 64, c2 * 64 + 64)
                        cs = slice(i * 128 + c2 * 64, i * 128 + c2 * 64 + 64)
                        for h in range(4):
                            act(Sp_bf[:, h, :], S_hg[:, h, :], AF.Copy, ["S_hg.%d" % h, "ebl.%d" % h], ["Sp_bf.%d" % h], scale=ebl[:, h, c:c + 1])
                        psb, psbn = acc[0], "acc0"
                        for h in range(4):
                            hv = slice(h * 128, (h + 1) * 128)
                            mm(po[pr, hv], sTm[pr, i, h, :], v_tok[pr, i, hv], True, False, ["sTm.%d" % i, "v_tok.%d" % i], [pon], inc=False)
                            mm(po[pr, hv], qTt[:, h, cs], Sp_bf[:, h, :], False, True, ["qTt.%d" % h, "Sp_bf.%d" % h], [pon], inc=False)
                            mm(psb[:, hv], ktok[pr, i, hv], v_tok[pr, i, hv], True, True, ["ktok.%d" % i, "v_tok.%d" % i], [psbn], inc=(h == 3))
                        for h in range(4):
                            hv = slice(h * 128, (h + 1) * 128)
                            stt(S_hg[:, h, :], S_hg[:, h, :], ebl[:, h, c:c + 1], psb[:, hv], ALU.mult, ALU.add, ["S_hg.%d" % h, "ebl.%d" % h, psbn], ["S_hg.%d" % h])
                        if c2 == 1:
                            cp("act", osb[:, 0:512], po[:, :], [pon], [osn])
                    return r_hg
                post.append(mk_hg(0)); post.append(mk_hg(1))

                def mk_gd(c2):
                    def r_gd():
                        pu, pun = rcA, "rcA"
                        poa, poan = rcB, "rcB"
                        pob, pobn = acc[1], "acc1"
                        pss, pssn = acc[0], "acc0"
                        pr = slice(c2 * 64, c2 * 64 + 64)
                        cs = slice(i * 128 + c2 * 64, i * 128 + c2 * 64 + 64)
                        for h in range(4):
                            hv = slice(h * 128, (h + 1) * 128)
                            mm(pu[pr, hv], qf[pr, h, :], rv[pr, h, 0:128], True, False, [qfn, rvn], [pun], inc=False)
                            mm(pu[pr, hv], nw[:, h, c2 * 64:(c2 + 1) * 64], Sg_bf[:, h, :], False, True, [nwn, "Sg_bf.%d" % h], [pun], inc=False)
                            mm(pob[pr, hv], qcT[:, h, cs], Sg_bf[:, h, :], True, True, ["qcT.%d" % h, "Sg_bf.%d" % h], [pobn], inc=(h == 3))
                        cp("act", u_bf[pr, :, :], pu[pr, :].rearrange("p (h v) -> p h v", h=4), [pun, "u_bf"], ["u_bf"])
                        for h in range(4):
                            hv = slice(h * 128, (h + 1) * 128)
                            mm(poa[pr, hv], at[pr, h, :], u_bf[pr, h, :], True, True, [atn, "u_bf"], [poan], inc=False)
                            mm(pss[:, hv], kd[pr, hv], u_bf[pr, h, :], True, True, [kdn, "u_bf"], [pssn], inc=(h == 3))
                        for h in range(4):
                            hv = slice(h * 128, (h + 1) * 128)
                            stt(S_gd[:, h, :], S_gd[:, h, :], eGl[:, i, c2, h:h + 1], pss[:, hv], ALU.mult, ALU.add, ["S_gd.%d" % h, "eGl", pssn], ["S_gd.%d" % h])
                            cp("act", Sg_bf[:, h, :], S_gd[:, h, :], ["S_gd.%d" % h], ["Sg_bf.%d" % h])
                        if c2 == 1:
                            cp("act", otmp[:, 0:512], poa[:, :], [poan], ["otmp"])
                            for h in range(4):
                                hv = slice(h * 128, (h + 1) * 128)
                                stt(osb[:, 512 + h * 128:512 + (h + 1) * 128], pob[:, hv], f_o[:, i, h:h + 1], otmp[:, hv], ALU.mult, ALU.add,
                                    [pobn, "f_o", "otmp", osn], [osn])
                            if "mix" in dbg:
                                S.dma("sp", "dbg%d" % par, dbg_out["d_o"][tok0:tok0 + 128, :], osb[:], reads=[osn])
                                S.wait_all("sp", [osn])
                    return r_gd
                post.append(mk_gd(0)); post.append(mk_gd(1))

                def o_norm():
                    S.dma("sp", "xr", xr[:], xp[tok0:tok0 + 128, :], writes=["xr"])
                    for hh in range(8):
                        act(ybf[:, 0:128], osb[:, hh * 128:(hh + 1) * 128], AF.Square, [osn], ["ybf", "ssq8"], accum_out=ssq8[:, hh:hh + 1])
                    rstd_from_ssq(ssq8[:], 128, 128, "ssq8")
                    for hh in range(8):
                        stt(ybf[:, hh * 128:(hh + 1) * 128], osb[:, hh * 128:(hh + 1) * 128], ssq8[:, hh:hh + 1], sz[:, i, hh * 128:(hh + 1) * 128],
                            ALU.mult, ALU.mult, [osn, "ssq8", "sz.%d.0" % i, "sz.%d.1" % i, "ybf"], ["ybf"])
                post.append(o_norm)

                def o_proj():
                    pt, ptn = next_ps("trp")
                    for j in range(8):
                        tr(pt[:, j * 128:(j + 1) * 128], ybf[:, j * 128:(j + 1) * 128], ident_bf[:], ["ybf", "ident_bf"], [ptn], inc=(j == 7))
                    cp("dve", yT[:].rearrange("p j t -> p (j t)"), pt[:, :], [ptn], ["yT"])
                    for half in range(2):
                        pa, pan = next_ps("pp")
                        for j in range(8):
                            mm(pa[:, :], yT[:, j, :], W2[:, j, half * 512:(half + 1) * 512], j == 0, j == 7, ["yT", "W2.%d" % j], [pan], inc=(j == 7))
                        tt("dve", xr[:, half * 512:(half + 1) * 512], pa[:, :], xr[:, half * 512:(half + 1) * 512], ALU.add, [pan, "xr"], ["xr"])
                post.append(o_proj)

                def o_fin():
                    act(ybf[:], xr[:], AF.Square, ["xr"], ["ybf", "ssqf"], accum_out=ssqf[:])
                    rstd_from_ssq(ssqf[:], 128, D, "ssqf")
                    stt(xr[:], xr[:], ssqf[:], fin_bc[:], ALU.mult, ALU.mult, ["xr", "ssqf", "fin_bc"], ["xr"])
                    S.dma("sp", "yo", yp[tok0:tok0 + 128, :], xr[:], reads=["xr"])
                post.append(o_fin)
                return prep, post

            stages = [tile_stages(i) for i in range(4)]
            for f in stages[0][0]:
                f()
            for i in range(5):
                lists = []
                if i + 1 <= 3:
                    lists.append(stages[i + 1][0])
                if i <= 3:
                    lists.append(stages[i][1][0:4])
                if i >= 1:
                    lists.append(stages[i - 1][1][4:])
                if blk + 1 < NBLK and i < 4:
                    lists.append([(lambda i=i: phase_x(blk + 1, i, xt2, ["e_sb", "l1"], "xt2"))])
                pos = [0] * len(lists)
                total = sum(len(l) for l in lists)
                for _ in range(total):
                    best = None
                    for k_, l in enumerate(lists):
                        if pos[k_] < len(l):
                            frac = pos[k_] / float(len(l))
                            if best is None or frac < best[0]:
                                best = (frac, k_)
                    k_ = best[1]
                    lists[k_][pos[k_]]()
                    pos[k_] += 1
        S.dma("sp", "o_shg", o_shg.rearrange("h k v -> k h v"), S_hg[:], reads=["S_hg"] + H4("S_hg"))
        S.dma("sp", "o_sgd", o_sgd.rearrange("h k v -> k h v"), S_gd[:], reads=["S_gd"] + H4("S_gd"))
        outs_done += ["S_hg", "S_gd", "xr"]


        NB_ = NS
        S.dma("sp", "xr", xr[0:NB_, :], xs, writes=["xr"])
        act(hbf[0][0:NB_, :], xr[0:NB_, :], AF.Square, ["xr"], ["hbf0", "ssq0"], accum_out=ssq[0][0:NB_, :])
        rstd_from_ssq(ssq[0][0:NB_, :], NB_, D, "ssq0")
        ts("dve", hbf[0][0:NB_, :], xr[0:NB_, :], ssq[0][0:NB_, :], None, ALU.mult, None, ["xr", "ssq0"], ["hbf0"])
        pt, ptn = next_ps("trp")
        for j in range(8):
            tr(pt[:, j * NB_:(j + 1) * NB_], hbf[0][0:NB_, j * 128:(j + 1) * 128], ident_bf[0:NB_, 0:NB_], ["hbf0", "ident_bf"], [ptn], inc=(j == 7))
        hsT = hTb[:, :, 0:NB_]
        cp("act", hsT, pt[:, 0:8 * NB_].rearrange("p (j t) -> p j t", j=8), [ptn], ["hT"])
        PT = e_sb[:].rearrange("p (t b) -> p t b", b=NB_)
        pa, pan = next_ps("acc")
        for ctl in range(32):
            for j in range(8):
                mm(pa[:, ctl * NB_:(ctl + 1) * NB_], W1[:, j, ctl * 128:(ctl + 1) * 128], hsT[:, j, :], j == 0, j == 7,
                   ["hT"] + w1res(ctl * 128, ctl * 128 + 128), [pan], inc=(ctl == 31 and j == 7))
        cp("dve", e_sb[:], pa[:, :], [pan], ["e_sb"])
        bgT = l1[0:8, 0:NB_]
        pa2, pa2n = next_ps("acc")
        for j in range(8):
            mm(pa2[0:8, 0:NB_], W1[:, j, C_GB:C_GB + 8], hsT[:, j, :], j == 0, j == 7, ["hT"] + w1res(C_GB, C_GB + 8), [pa2n], inc=(j == 7))
        cp("act", bgT, pa2[0:8, 0:NB_], [pa2n], ["l1"])
        for n_ in range(3):
            pa3, pa3n = next_ps("acc")
            for j in range(8):
                mm(pa3[0:NB_, :], hsT[:, j, :], W1[:, j, C_GQ + n_ * 512:C_GQ + (n_ + 1) * 512], j == 0, j == 7,
                   ["hT"] + w1res(C_GQ + n_ * 512, C_GQ + (n_ + 1) * 512), [pa3n], inc=(j == 7))
            cp("act" if n_ % 2 == 0 else "dve", bcum[0:NB_, :], pa3[0:NB_, :], [pa3n], ["bcum"])
            S.dma("sp", "os_cv_new", os_cv[:, 2, n_ * 512:(n_ + 1) * 512], bcum[0:NB_, :], reads=["bcum"])
            S.wait_all("sp", ["bcum"])
        S.dma("sp", "os_cv_pass", os_cv[:, 0:2, :], scv[:, 1:3, :], writes=["os_cv_pass_r"])
        outs_done.append("os_cv_pass_r")
        scv_f = scv.rearrange("b j c -> (b j) c")
        S.dma("sp", "o_sb0", o_sb[0][0:48, :], scv_f[:, 0:1024], writes=["o_sb0"])
        S.dma("sp", "o_sb1", o_sb[1][0:48, 0:512], scv_f[:, 1024:1536], writes=["o_sb1"])
        scvT = otmp[:, 0:576].rearrange("p (t q) -> p t q", q=48)
        for grp in range(2):
            pm_, pmn_ = next_ps("mx")
            cts = range(0, 8) if grp == 0 else range(8, 12)
            for ct in cts:
                src = o_sb[0][0:48, ct * 128:(ct + 1) * 128] if ct < 8 else o_sb[1][0:48, (ct - 8) * 128:(ct - 7) * 128]
                tr(pm_[:, (ct - cts[0]) * 48:(ct - cts[0] + 1) * 48], src, ident_f[0:48, 0:48], ["o_sb0", "o_sb1", "ident_f"], [pmn_], inc=(ct == cts[-1]))
            n_ct = len(cts)
            cp("dve", scvT[:, cts[0]:cts[0] + n_ct, :], pm_[:, 0:n_ct * 48].rearrange("p (t q) -> p t q", q=48), [pmn_, "otmp"], ["otmp"])
        scv4 = otmp[:, 0:576].rearrange("p (t b j) -> p t b j", b=NB_, j=3)
        cva = l2[:, 0:192].rearrange("p (t b) -> p t b", b=NB_)
        cvb = l2[:, 192:384].rearrange("p (t b) -> p t b", b=NB_)
        qkvc = dd[:, 0:192].rearrange("p (t b) -> p t b", b=NB_)
        tt(P, cva, scv4[:, :, :, 0], bc(cw_c[:, :, 0:1], [128, 12, NB_]), ALU.mult, ["otmp", "cw_c"], ["l2"])
        for j_ in (1, 2):
            tt(P, cvb, scv4[:, :, :, j_], bc(cw_c[:, :, j_:j_ + 1], [128, 12, NB_]), ALU.mult, ["otmp", "cw_c", "l2"], ["l2"])
            tt(P, cva, cva, cvb, ALU.add, ["l2"], ["l2"])
        tt(P, cvb, PT[:, 16:28, :], bc(cw_c[:, :, 3:4], [128, 12, NB_]), ALU.mult, ["e_sb", "cw_c", "l2"], ["l2"])
        tt(P, cva, cva, cvb, ALU.add, ["l2"], ["l2"])
        act(qkvc, cva, AF.Silu, ["l2"], ["dd"])
        szT = Sp_bf[:].rearrange("p h v -> p (h v)")[:, 0:8 * NB_].rearrange("p (j b) -> p j b", b=NB_)
        act(szT[:, 0:4, :], PT[:, 12:16, :], AF.Silu, ["e_sb"], ["Sp_bf"] + H4("Sp_bf"))
        act(szT[:, 4:8, :], PT[:, 28:32, :], AF.Silu, ["e_sb", "Sp_bf"], ["Sp_bf"])
        R1f = R1[:].rearrange("p h s -> p (h s)"); R2f = R2[:].rearrange("p h s -> p (h s)")
        R3f = R3[:].rearrange("p h s -> p (h s)"); R4f = R4[:].rearrange("p h s -> p (h s)")
        v3 = lambda ap: ap.rearrange("p (h b) -> p h b", b=NB_)
        e_s = v3(R1f[:, 0:64]); num_s = v3(R1f[:, 64:128]); den_s = v3(R1f[:, 128:192]); fg_s = v3(R1f[:, 192:256])
        kk_s = v3(R2f[:, 0:64])
        act(e_s, PT[:, 4:8, :], AF.Exp, ["e_sb"], ["R1"], scale=-1.0)
        tt("dve", num_s, e_s, bc(lb_c[:].unsqueeze(2), [128, 4, NB_]), ALU.mult, ["R1", "lb_c"], ["R1"])
        ts("dve", num_s, num_s, 1.0, None, ALU.add, None, ["R1"], ["R1"])
        ts("dve", den_s, e_s, 1.0, None, ALU.add, None, ["R1"], ["R1"])
        S.op("dve", "reciprocal", den_s, den_s, reads=["R1"], writes=["R1"])
        tt("dve", fg_s, num_s, den_s, ALU.mult, ["R1"], ["R1"])
        ts("dve", kk_s, fg_s, -1.0, 1.0, ALU.mult, ALU.add, ["R1"], ["R2"])
        sq_s = R2f[:, 64:192]
        tt("dve", sq_s, dd[:, 0:128], dd[:, 0:128], ALU.mult, ["dd", "R2"], ["R2"])
        pm_, pmn_ = next_ps("mx")
        mm(pm_[:, 0:128], ones_f[:], sq_s, True, True, ["ones_f", "R2"], [pmn_])
        rn_s = R3f[:, 0:128]
        act(rn_s, pm_[:, 0:128], AF.Ln, [pmn_, "eps_c"], ["R3"], bias=eps_c[:, :])
        act(rn_s, rn_s, AF.Exp, ["R3"], ["R3"], scale=-0.5)
        qkn = R3f[:, 128:256]
        tt("dve", qkn, dd[:, 0:128], rn_s, ALU.mult, ["dd", "R3"], ["R3"])
        ts("dve", qkn[:, 0:64], qkn[:, 0:64], float(128.0 ** -0.5), None, ALU.mult, None, ["R3"], ["R3"])
        qn_s = v3(qkn[:, 0:64]); kn_s = v3(qkn[:, 64:128]); vc_s = qkvc[:, 8:12, :]
        rall = R4f[0:8, 0:128].rearrange("p (r b) -> p r b", b=NB_)
        tt("dve", rall, bc(bgT.unsqueeze(1), [8, 8, NB_]), bc(ident_f[0:8, 0:8].unsqueeze(2), [8, 8, NB_]), ALU.mult, ["l1", "ident_f"], ["R4"])
        pm2, pm2n = next_ps("mx")
        mm(pm2[:, 0:128], ones_f[0:8, :], R4f[0:8, 0:128], True, True, ["ones_f", "R4"], [pm2n])
        bgb = tmpA[:].rearrange("p h s -> p (h s)")
        beta_s = v3(bgb[:, 0:64]); eg_s = v3(bgb[:, 64:128]); tsm = v3(bgb[:, 128:192])
        act(beta_s, pm2[:, 0:64].rearrange("p (h b) -> p h b", b=NB_), AF.Exp, [pm2n], ["tmpA"], scale=-1.0)
        act(beta_s, beta_s, AF.Ln, ["tmpA"], ["tmpA"], bias=1.0)
        act(beta_s, beta_s, AF.Exp, ["tmpA"], ["tmpA"], scale=-1.0)
        tt("dve", tsm, pm2[:, 64:128].rearrange("p (h b) -> p h b", b=NB_), bc(dtb_bc[:].unsqueeze(2), [128, 4, NB_]), ALU.add, [pm2n, "dtb_bc", "tmpA"], ["tmpA"])
        act(tsm, tsm, AF.Exp, ["tmpA"], ["tmpA"])
        act(tsm, tsm, AF.Ln, ["tmpA"], ["tmpA"], bias=1.0)
        tt("dve", tsm, tsm, bc(negA_bc[:].unsqueeze(2), [128, 4, NB_]), ALU.mult, ["tmpA", "negA_bc"], ["tmpA"])
        act(eg_s, tsm, AF.Exp, ["tmpA"], ["tmpA"])
        Shg = [S_hg, LL[:].rearrange("p (h v) -> p h v", h=4), cacc[1][:].rearrange("p (h v) -> p h v", h=4)]; Shn = ["S_hg", "LL", "cacc1"]
        Sgd = [S_gd, cacc[0][:].rearrange("p (h v) -> p h v", h=4), otmp[:, 512:1024].rearrange("p (h v) -> p h v", h=4)]; Sgn = ["S_gd", "cacc0", "otmp"]
        diag4 = pre[0][:, 0:512].rearrange("p (h v) -> p h v", h=4); tbuf = pre[1][:, 0:512].rearrange("p (h v) -> p h v", h=4)
        diag4G = o_sb[0][:, 0:512].rearrange("p (h v) -> p h v", h=4); tbufG = o_sb[1][:, 0:512].rearrange("p (h v) -> p h v", h=4)
        dHh = qTt[:, :, 0:128]; dHl = qTt[:, :, 128:256]; dGh = kTt[:, :, 0:128]; dGl = kTt[:, :, 128:256]
        vhA = vcT[:, :, 0:NB_]; vlA = vcT[:, :, NB_:2 * NB_]; dh1 = vcT[:, :, 2 * NB_:2 * NB_ + 1]; dl1 = vcT[:, :, 2 * NB_ + 1:2 * NB_ + 2]
        idbb = bc(ident_bf[:].unsqueeze(1), [128, 4, 128])
        dlt = v3(tmpB[:].rearrange("p h s -> p (h s)")[:, 0:64])
        idb = bc(ident_f[:].unsqueeze(1), [128, 4, 128])
        hq_s = PT[:, 0:4, :]; hi_s = PT[:, 8:12, :]
        cp("dve", vhA, hi_s, ["e_sb"], H4("vcT"))
        tt("dve", vlA, hi_s, vhA, ALU.subtract, ["e_sb"] + H4("vcT"), H4("vcT"))

        def sview(t, k):
            return t[k][:] if k == 0 else t[k]
        def load_states(b):
            k_ = b % 3
            S.dma("sp", "in_" + Shn[k_], sview(Shg, k_), shg[b].rearrange("h k v -> k h v"), writes=[Shn[k_]])
            S.dma("act", "in_" + Sgn[k_], sview(Sgd, k_), sgd[b].rearrange("h k v -> k h v"), writes=[Sgn[k_]])

        def tok_bufs(b):
            k_ = b % 3
            bH, bHn = (rcA, "rcA") if b % 2 == 0 else (pp[1], "pp1")
            bG, bGn = (rcB, "rcB") if b % 2 == 0 else (acc[1], "acc1")
            return sview(Shg, k_), Shn[k_], sview(Sgd, k_), Sgn[k_], bH, bHn, bG, bGn

        def phase1(b):
            sh, shn, sg, sgn, bH, bHn, bG, bGn = tok_bufs(b)
            pk_, pkn_ = acc[0], "acc0"
            for h in range(4):
                mm(pk_[:, h:h + 1], sg[:, h, :], kn_s[:, h, b:b + 1], True, True, [sgn, "R3"], [pkn_], inc=(h == 3))
            tt(P, dHh, idbb, bc(vhA[:, :, b:b + 1], [128, 4, 128]), ALU.mult, ["ident_bf"] + H4("vcT"), H4("qTt"))
            tt(P, dHl, idbb, bc(vlA[:, :, b:b + 1], [128, 4, 128]), ALU.mult, ["ident_bf"] + H4("vcT") + H4("qTt"), H4("qTt"))
            tt(P, sh, sh, bc(fg_s[:, :, b:b + 1], [128, 4, 128]), ALU.mult, [shn, "R1"], [shn])
            for h in range(4):
                mm(bH[:, h * 128:(h + 1) * 128], ones_bf[:], dHh[:, h, :], True, False, ["ones_bf"] + H4("qTt"), [bHn], inc=False)
                mm(bH[:, h * 128:(h + 1) * 128], ones_bf[:], dHl[:, h, :], False, True, ["ones_bf"] + H4("qTt"), [bHn], inc=(h == 3))
            tt("dve", dlt[:, :, 0], pk_[:, 0:4], eg_s[:, :, b], ALU.mult, [pkn_, "tmpA"], ["tmpB"])
            tt("dve", dlt[:, :, 0], vc_s[:, :, b], dlt[:, :, 0], ALU.subtract, ["dd", "tmpB"], ["tmpB"])
            tt("dve", dlt[:, :, 0], dlt[:, :, 0], beta_s[:, :, b], ALU.mult, ["tmpB", "tmpA"], ["tmpB"])
            cp("dve", dh1, dlt[:, :, 0:1], ["tmpB"] + H4("vcT"), H4("vcT"))
            tt("dve", dl1, dlt[:, :, 0:1], dh1, ALU.subtract, ["tmpB"] + H4("vcT"), H4("vcT"))
            tt(P, dGh, idbb, bc(dh1, [128, 4, 128]), ALU.mult, ["ident_bf"] + H4("vcT"), H4("kTt"))
            tt(P, dGl, idbb, bc(dl1, [128, 4, 128]), ALU.mult, ["ident_bf"] + H4("vcT") + H4("kTt"), H4("kTt"))
            tt(P, sg, sg, bc(eg_s[:, :, b:b + 1], [128, 4, 128]), ALU.mult, [sgn, "tmpA"], [sgn])
            for h in range(4):
                mm(bG[:, h * 128:(h + 1) * 128], ones_bf[:], dGh[:, h, :], True, False, ["ones_bf"] + H4("kTt"), [bGn], inc=False)
                mm(bG[:, h * 128:(h + 1) * 128], ones_bf[:], dGl[:, h, :], False, True, ["ones_bf"] + H4("kTt"), [bGn], inc=(h == 3))

        def phase2(b):
            sh, shn, sg, sgn, bH, bHn, bG, bGn = tok_bufs(b)
            tt("dve", tbuf, bH[:, :].rearrange("p (h v) -> p h v", h=4), bc(kk_s[:, :, b:b + 1], [128, 4, 128]), ALU.mult, [bHn, "R2"], ["pre1"])
            tt("dve", sh, sh, tbuf, ALU.add, [shn, "pre1"], [shn])
            S.dma("sp", "out_" + shn, os_hg[b].rearrange("h k v -> k h v"), sh, reads=[shn])
            for h in range(4):
                mm(pp[0][:, h * NB_ + b:h * NB_ + b + 1], sh[:, h, :], hq_s[:, h, b:b + 1], True, True, [shn, "e_sb"], ["pp0"], inc=(h == 3))
            tt("dve", tbufG, bG[:, :].rearrange("p (h v) -> p h v", h=4), bc(kn_s[:, :, b:b + 1], [128, 4, 128]), ALU.mult, [bGn, "R3"], ["o_sb1"])
            tt("dve", sg, sg, tbufG, ALU.add, [sgn, "o_sb1"], [sgn])
            S.dma("act", "out_" + sgn, os_gd[b].rearrange("h k v -> k h v"), sg, reads=[sgn])
            for h in range(4):
                mm(pp[0][:, 64 + h * NB_ + b:64 + h * NB_ + b + 1], sg[:, h, :], qn_s[:, h, b:b + 1], True, True, [sgn, "R3"], ["pp0"], inc=(h == 3))

        load_states(0)
        load_states(1)
        phase1(0)
        for b in range(NB_):
            if b + 2 < NB_:
                load_states(b + 2)
            if b + 1 < NB_:
                phase1(b + 1)
            phase2(b)
        outs_done += Shn + Sgn
        oT_s = bcum[:, 0:128]
        cp("dve", oT_s, pp[0][:, 0:128], ["pp0"], ["bcum"])
        sq2 = bcum[:, 128:256]
        tt("dve", sq2, oT_s, oT_s, ALU.mult, ["bcum"], ["bcum"])
        pm3, pm3n = next_ps("mx")
        mm(pm3[:, 0:128], ones_f[:], sq2, True, True, ["ones_f", "bcum"], [pm3n])
        rs2 = bcum[:, 256:384]
        act(rs2, pm3[:, 0:128], AF.Ln, [pm3n, "eps_c"], ["bcum"], bias=eps_c[:, :], scale=1.0 / 128)
        act(rs2, rs2, AF.Exp, ["bcum"], ["bcum"], scale=-0.5)
        tt("dve", oT_s, oT_s, rs2, ALU.mult, ["bcum"], ["bcum"])
        yTs = u_bf[:].rearrange("p h v -> p (h v)")[:, 0:128]
        tt("dve", yTs, oT_s, szT.rearrange("p j b -> p (j b)"), ALU.mult, ["bcum", "Sp_bf"], ["u_bf"])
        yT3 = yTs.rearrange("p (j b) -> p j b", b=NB_)
        for half in range(2):
            pa, pan = next_ps("acc")
            for j in range(8):
                mm(pa[0:NB_, :], yT3[:, j, :], W2[:, j, half * 512:(half + 1) * 512], j == 0, j == 7, ["u_bf", "W2.%d" % j], [pan], inc=(j == 7))
            tt("dve", xr[0:NB_, half * 512:(half + 1) * 512], pa[0:NB_, :], xr[0:NB_, half * 512:(half + 1) * 512], ALU.add, [pan, "xr"], ["xr"])
        act(ybf[0:NB_, :], xr[0:NB_, :], AF.Square, ["xr"], ["ybf", "ssqf"], accum_out=ssqf[0:NB_, :])
        rstd_from_ssq(ssqf[0:NB_, :], NB_, D, "ssqf")
        stt(xr[0:NB_, :], xr[0:NB_, :], ssqf[0:NB_, :], fin_bc[0:NB_, :], ALU.mult, ALU.mult, ["xr", "ssqf", "fin_bc"], ["xr"])
        S.dma("sp", "ys", ys, xr[0:NB_, :], reads=["xr"])
        outs_done += ["xr"]

        S.wait_all("sp", outs_done)
        S.emit()
    return nc


_NC_CACHE = {}


def kernel(x_prompt, x_sample, state_hgrn, state_gdn, state_gdn_conv, norm_w, w_in, hg_lb_logits, conv_w,
           gdn_a_log, gdn_dt_bias, hg_out_norm, gdn_out_norm, w_out, final_norm, _dbg=None):
    f = lambda a: np.ascontiguousarray(np.asarray(a, dtype=np.float32))
    key = tuple(sorted(_dbg)) if _dbg else ()
    if key not in _NC_CACHE:
        _NC_CACHE[key] = build(_dbg)
    nc = _NC_CACHE[key]
    in_maps = []
    for c in range(NCORE):
        sl = slice(c * NS, (c + 1) * NS)
        in_maps.append({
            "xp": f(x_prompt[c]), "xs": f(x_sample[sl, 0]),
            "shg": f(state_hgrn[0, sl]), "sgd": f(state_gdn[0, sl]), "scv": f(state_gdn_conv[0, sl]),
            "norm_w": f(norm_w[0]), "w_in": f(w_in[0]), "lbl": f(hg_lb_logits), "conv_w": f(conv_w[0]),
            "a_log": f(gdn_a_log[0]), "dt_bias": f(gdn_dt_bias[0]), "hgn": f(hg_out_norm[0]), "gdnn": f(gdn_out_norm[0]),
            "w_out": f(w_out[0]), "fin": f(final_norm),
        })
    res = run_bass_kernel_spmd(nc, in_maps, core_ids=list(range(NCORE)))
    r = res.results
    if _dbg:
        return r
    y_prompt = np.stack([r[c]["yp"] for c in range(NCORE)], 0)
    y_sample = np.concatenate([r[c]["ys"] for c in range(NCORE)], 0)[:, None, :]
    nhp = np.stack([r[c]["o_shg"] for c in range(NCORE)], 0)[None]
    ngp = np.stack([r[c]["o_sgd"] for c in range(NCORE)], 0)[None]
    ncp = np.stack([r[c]["o_cv"] for c in range(NCORE)], 0)[None]
    nhs = np.concatenate([r[c]["os_hg"] for c in range(NCORE)], 0)[None]
    ngs = np.concatenate([r[c]["os_gd"] for c in range(NCORE)], 0)[None]
    ncs = np.concatenate([r[c]["os_cv"] for c in range(NCORE)], 0)[None]
    return (y_prompt, y_sample, nhp, ngp, ncp, nhs, ngs, ncs)
```

```python
import contextlib
import numpy as np
import concourse.bass as bass
import concourse.mybir as mybir
from concourse.bass_utils import run_bass_kernel_spmd

F32 = mybir.dt.float32
BF16 = mybir.dt.bfloat16
ALU = mybir.AluOpType
AF = mybir.ActivationFunctionType
AX = mybir.AxisListType

ENGS = ("pe", "act", "dve", "pool", "sp")
PSUM_PREFIXES = ("acc", "trp", "rc", "pp")
T = 2048
D = 1024
DIN = 4104
NCORE = 8
NS = 16
TB = 512
NBLK = T // TB
EPS = 1e-6
NEG = -30000.0


class Sched:
    def __init__(self, nc):
        self.nc = nc
        self.ops = {e: [] for e in ENGS}
        self.cnt = {e: 0 for e in ENGS}
        self.res = {}
        self.seen = {e: {} for e in ENGS}
        self.dma_cnt = {}
        self.sem_names = set(ENGS)
        self.unwritten = set()

    def _need(self, eng, ev, waits):
        if ev is None:
            return
        s, v = ev
        if s == "pe" and eng == "pe":
            return
        if s in self.dma_cnt and v != self.dma_cnt[s]:
            raise RuntimeError("DMA sem %s: wait for %d but %d issued" % (s, v, self.dma_cnt[s]))
        if self.seen[eng].get(s, 0) >= v:
            return
        waits[s] = max(waits.get(s, 0), v)

    def _deps(self, eng, reads, writes):
        waits = {}
        for k in reads:
            r = self.res.get(k)
            if r:
                self._need(eng, r["w"], waits)
            elif not k.startswith(PSUM_PREFIXES):
                self.unwritten.add(k)
        for k in writes:
            r = self.res.get(k)
            if r:
                self._need(eng, r["w"], waits)
                for ev in r["r"]:
                    self._need(eng, ev, waits)
        for s, v in waits.items():
            self.seen[eng][s] = v
        return waits

    def _commit(self, ev, reads, writes):
        for k in reads:
            self.res.setdefault(k, {"w": None, "r": []})["r"].append(ev)
        for k in writes:
            self.res[k] = {"w": ev, "r": []}

    def op(self, eng, meth, *args, reads=(), writes=(), inc=True, after=None, **kw):
        fn = (meth, args, kw)
        writes = list(writes) + [k for k in reads if k.startswith(PSUM_PREFIXES)]
        waits = self._deps(eng, reads, writes)
        if after is not None:
            waits[after[0]] = max(waits.get(after[0], 0), after[1])
        if inc:
            self.cnt[eng] += 1
            ev = (eng, self.cnt[eng])
        else:
            ev = (eng, self.cnt[eng] + 1)
        self.ops[eng].append((waits, fn, (eng, 1) if inc else None))
        self._commit(ev, reads, writes)
        return ev

    def dma(self, eng, sem, out, in_, reads=(), writes=(), **kw):
        fn = ("dma_start", (), dict(out=out, in_=in_, **kw))
        self.sem_names.add(sem)
        self.dma_cnt.setdefault(sem, 0)
        waits = self._deps(eng, reads, writes)
        self.dma_cnt[sem] += 16
        ev = (sem, self.dma_cnt[sem])
        self.ops[eng].append((waits, fn, (sem, 16)))
        self._commit(ev, reads, writes)
        return ev

    def wait_all(self, eng, keys):
        waits = {}
        for k in keys:
            r = self.res.get(k)
            if r:
                self._need(eng, r["w"], waits)
                for ev in r["r"]:
                    self._need(eng, ev, waits)
        self.ops[eng].append((waits, None, None))

    def emit(self):
        nc = self.nc
        names = sorted(self.sem_names)
        with contextlib.ExitStack() as st:
            sems = {n: st.enter_context(nc.semaphore("s_" + n)) for n in names}
            block = st.enter_context(nc.Block())

            def run(engname):
                def body(e):
                    for waits, fn, inc in self.ops[engname]:
                        for s, v in waits.items():
                            e.wait_ge(sems[s], v)
                        if fn is None:
                            continue
                        ins = getattr(e, fn[0])(*fn[1], **fn[2])
                        if inc is not None:
                            ins.then_inc(sems[inc[0]], inc[1])
                return body

            block.tensor(run("pe"))
            block.scalar(run("act"))
            block.vector(run("dve"))
            block.gpsimd(run("pool"))
            block.sync(run("sp"))


C_HQ, C_HF, C_HI, C_HZ, C_GQ, C_GK, C_GV, C_GZ, C_GB = 0, 512, 1024, 1536, 2048, 2560, 3072, 3584, 4096


def build(dbg=None):
    dbg = dbg or set()
    nc = bass.Bass("TRN2", target_bir_lowering=False)

    def din(name, shape):
        return nc.dram_tensor(name, shape, F32, kind="ExternalInput").ap()

    def dout(name, shape):
        return nc.dram_tensor(name, shape, F32, kind="ExternalOutput").ap()

    xp = din("xp", [T, D]); xs = din("xs", [NS, D])
    shg = din("shg", [NS, 4, 128, 128]); sgd = din("sgd", [NS, 4, 128, 128]); scv = din("scv", [NS, 3, 1536])
    norm_w = din("norm_w", [D]); w_in = din("w_in", [D, DIN]); lbl = din("lbl", [2, 512])
    conv_w = din("conv_w", [4, 1536]); a_log = din("a_log", [4]); dt_bias = din("dt_bias", [4])
    hgn = din("hgn", [128]); gdnn = din("gdnn", [128]); w_out = din("w_out", [D, D]); fin = din("fin", [D])
    yp = dout("yp", [T, D]); ys = dout("ys", [NS, D])
    o_shg = dout("o_shg", [4, 128, 128]); o_sgd = dout("o_sgd", [4, 128, 128]); o_cv = dout("o_cv", [3, 1536])
    os_hg = dout("os_hg", [NS, 4, 128, 128]); os_gd = dout("os_gd", [NS, 4, 128, 128]); os_cv = dout("os_cv", [NS, 3, 1536])
    dbg_out = {}
    if "mix" in dbg:
        dbg_out["d_o"] = dout("d_o", [T, D])

    with contextlib.ExitStack() as st:
        def sb(name, shape, dt=F32):
            return st.enter_context(nc.sbuf_tensor(name, shape, dt))

        def ps(name, shape, dt=F32):
            return st.enter_context(nc.psum_tensor(name, shape, dt))

        S = Sched(nc)
        outs_done = []

        acc = [ps("acc%d" % i, [128, 512]) for i in range(2)]
        trp = [ps("trp%d" % i, [128, 1024], BF16) for i in range(2)]
        rcA = ps("rcA", [128, 512]); rcB = ps("rcB", [128, 512])
        pp = [ps("pp%d" % i, [128, 512]) for i in range(2)]
        mx = [rcA, rcB]
        misc = rcA
        rr = {"acc": 0, "trp": 0, "mx": 0, "pp": 0}
        PSN = {"acc": ["acc0", "acc1"], "trp": ["trp0", "trp1"], "mx": ["rcA", "rcB"], "pp": ["pp0", "pp1"]}

        def next_ps(kind):
            lst = {"acc": acc, "trp": trp, "mx": mx, "pp": pp}[kind]
            i = rr[kind] % len(lst)
            rr[kind] += 1
            return lst[i], PSN[kind][i]

        def act(out, in_, func, r, w, **kw):
            S.op("act", "activation", out, in_, func, reads=r, writes=w, **kw)

        def tt(eng, out, in0, in1, op, r, w):
            S.op(eng, "tensor_tensor", out, in0, in1, op, reads=r, writes=w)

        def ts(eng, out, in0, s1, s2, op0, op1, r, w):
            if s2 is None:
                S.op(eng, "tensor_scalar", out, in0, s1, None, op0, reads=r, writes=w)
            else:
                S.op(eng, "tensor_scalar", out, in0, s1, s2, op0, op1, reads=r, writes=w)

        def stt(out, in0, scalar, in1, op0, op1, r, w):
            S.op("dve", "scalar_tensor_tensor", out, in0, scalar, in1, op0, op1, reads=r, writes=w)

        def cp(eng, out, in_, r, w):
            if eng == "act":
                S.op("act", "activation", out, in_, AF.Copy, reads=r, writes=w)
            else:
                S.op(eng, "tensor_copy", out, in_, reads=r, writes=w)

        def mm(out, lhsT, rhs, start, stop, r, w, inc=True, after=None):
            return S.op("pe", "matmul", out, lhsT=lhsT, rhs=rhs, start=start, stop=stop, reads=r, writes=w, inc=inc, after=after)

        def tr(out, in_, ident, r, w, inc=True):
            S.op("pe", "transpose", out, in_, ident, reads=r, writes=w, inc=inc)

        def bc(ap, shape):
            return ap.to_broadcast(shape)

        dumped = set()

        def dump(name, ap, shape, res):
            if "dump" not in dbg or name in dumped:
                return
            dumped.add(name)
            d = dout("dd_" + name, shape)
            S.dma("sp", "dd_" + name, d, ap, reads=res)
            S.wait_all("sp", res)

        ident_bf = sb("ident_bf", [128, 128], BF16)
        ident_f = sb("ident_f", [128, 128])
        ones_bf = sb("ones_bf", [128, 128], BF16)
        ones_f = sb("ones_f", [128, 128])
        eps_c = sb("eps_c", [128, 1])
        hm = sb("hm", [128, 2])
        lnk_c = sb("lnk_c", [128, 1])
        fin_bc = sb("fin_bc", [128, D])
        normw_col = sb("normw_col", [128, 8])
        W1 = sb("W1", [128, 8, DIN], BF16)
        W2 = sb("W2", [128, 8, D], BF16)
        wn_col = sb("wn_col", [128, 2])
        lbl_sb = sb("lbl_sb", [128, 2, 4])
        lb_c = sb("lb_c", [128, 4]); lnoml_c = sb("lnoml_c", [128, 4]); lbt = sb("lbt", [128, 4])
        cw_c = sb("cw_c", [128, 12, 4])
        alog_bc = sb("alog_bc", [128, 4]); negA_bc = sb("negA_bc", [128, 4]); dtb_bc = sb("dtb_bc", [128, 4])
        BD = sb("BD", [128, 128]); Tri_bd = sb("Tri_bd", [128, 128]); LE = Tri_bd; SU_bd = sb("SU_bd", [128, 128])
        TriLoc = sb("TriLoc", [128, 64]); SUloc = sb("SUloc", [128, 64]); Iloc = sb("Iloc", [128, 64])
        NEGs4 = sb("NEGs4", [128, 4, 64]); NEGTi4 = sb("NEGTi4", [128, 4, 64])
        cmask = sb("cmask", [128, TB], BF16)
        blkscr = sb("blkscr", [128, 5 * TB])
        e_sb = blkscr[:, 0:TB]; l1 = blkscr[:, TB:2 * TB]; l2 = blkscr[:, 2 * TB:3 * TB]; bcum = blkscr[:, 3 * TB:4 * TB]; dd = blkscr[:, 4 * TB:5 * TB]
        xt2 = blkscr[:, 0:2 * TB]

        P = "pool"
        S.op(P, "memset", ident_bf[:], 1.0, writes=["ident_bf"])
        S.op(P, "affine_select", ident_bf[:], ident_bf[:], pattern=[[-1, 128]], compare_op=ALU.is_equal, fill=0.0, base=0,
             channel_multiplier=1, reads=["ident_bf"], writes=["ident_bf"])
        S.op(P, "memset", ident_f[:], 1.0, writes=["ident_f"])
        S.op(P, "affine_select", ident_f[:], ident_f[:], pattern=[[-1, 128]], compare_op=ALU.is_equal, fill=0.0, base=0,
             channel_multiplier=1, reads=["ident_f"], writes=["ident_f"])
        S.op(P, "memset", ones_bf[:], 1.0, writes=["ones_bf"])
        S.op(P, "memset", ones_f[:], 1.0, writes=["ones_f"])
        S.op(P, "memset", eps_c[:], EPS, writes=["eps_c"])
        S.op(P, "memset", lnk_c[:], -0.5 * float(np.log(128.0)), writes=["lnk_c"])
        S.op(P, "memset", hm[:], 0.0, writes=["hm"])
        S.op(P, "memset", hm[0:64, 0:1], 1.0, reads=["hm"], writes=["hm"])
        S.op(P, "memset", hm[64:128, 1:2], 1.0, reads=["hm"], writes=["hm"])
        S.op(P, "memset", LE[:], 1.0, writes=["Tri_bd"])
        S.op(P, "affine_select", LE[:], LE[:], pattern=[[1, 128]], compare_op=ALU.is_ge, fill=0.0, base=0,
             channel_multiplier=-1, reads=["Tri_bd"], writes=["Tri_bd"])
        S.op(P, "memset", BD[:], 0.0, writes=["BD"])
        S.op(P, "memset", BD[0:64, 0:64], 1.0, reads=["BD"], writes=["BD"])
        S.op(P, "memset", BD[64:128, 64:128], 1.0, reads=["BD"], writes=["BD"])
        tt(P, Tri_bd[:], LE[:], BD[:], ALU.mult, ["Tri_bd", "BD"], ["Tri_bd"])
        tt(P, SU_bd[:], BD[:], Tri_bd[:], ALU.subtract, ["BD", "Tri_bd"], ["SU_bd"])
        tt(P, TriLoc[:], Tri_bd[:, 0:64], Tri_bd[:, 64:128], ALU.add, ["Tri_bd"], ["TriLoc"])
        tt(P, SUloc[:], SU_bd[:, 0:64], SU_bd[:, 64:128], ALU.add, ["SU_bd"], ["SUloc"])
        tt(P, Iloc[:], ident_f[:, 0:64], ident_f[:, 64:128], ALU.add, ["ident_f"], ["Iloc"])
        for h in range(4):
            ts(P, NEGs4[:, h, :], SUloc[:], -NEG, NEG, ALU.mult, ALU.add, ["SUloc"], ["NEGs4"])
            ts(P, NEGTi4[:, h, :], TriLoc[:], -NEG, NEG, ALU.mult, ALU.add, ["TriLoc"], ["NEGTi4"])
        S.op(P, "memset", cmask[:], 1.0, writes=["cmask"])
        S.op(P, "memset", cmask[:].rearrange("p (c t) -> p c t", t=64)[:, :, 0:1], 0.0, reads=["cmask"], writes=["cmask"])

        S.dma("sp", "c_rows0", blkscr[0:4, 0:1536], conv_w, writes=["e_sb", "l1", "l2"])
        S.dma("sp", "c_rows1", blkscr[0:2, 1536:2048], lbl, writes=["bcum"])
        S.dma("sp", "c_rows2", blkscr[0:8, 2048:2176], norm_w.rearrange("(j p) -> j p", p=128), writes=["dd"])
        for ct_ in range(12):
            tr(pp[0][:, ct_ * 4:(ct_ + 1) * 4], blkscr[0:4, ct_ * 128:(ct_ + 1) * 128], ident_f[0:4, 0:4], ["e_sb", "l1", "l2", "ident_f"], ["pp0"], inc=False)
        for h_ in range(4):
            tr(pp[0][:, 48 + h_ * 2:50 + h_ * 2], blkscr[0:2, 1536 + h_ * 128:1536 + (h_ + 1) * 128], ident_f[0:2, 0:2], ["bcum", "ident_f"], ["pp0"], inc=False)
        tr(pp[0][:, 56:64], blkscr[0:8, 2048:2176], ident_f[0:8, 0:8], ["dd", "ident_f"], ["pp0"])
        cp("dve", normw_col[:], pp[0][:, 56:64], ["pp0"], ["normw_col"])
        cp("dve", cw_c[:], pp[0][:, 0:48].rearrange("p (t j) -> p t j", j=4), ["pp0"], ["cw_c"])
        cp("dve", lbl_sb[:].rearrange("p r h -> p h r"), pp[0][:, 48:56].rearrange("p (h r) -> p h r", r=2), ["pp0"], ["lbl_sb"])
        S.dma("sp", "c_wn0", wn_col[:, 0:1], hgn.rearrange("(p o) -> p o", o=1), writes=["wn0"])
        S.dma("sp", "c_wn1", wn_col[:, 1:2], gdnn.rearrange("(p o) -> p o", o=1), writes=["wn1"])
        o_sb = [sb("o_sb%d" % i, [128, D]) for i in range(2)]
        xr = sb("xt0", [128, D]); otmp = sb("otmp", [128, D])

        w_in_v = w_in.rearrange("(j p) c -> p j c", p=128)
        wq = []

        def mk_w1(n_, c0, jp):
            def f():
                stg = [(o_sb[0], "o_sb0"), (o_sb[1], "o_sb1"), (otmp, "otmp")]
                buf, bn = stg[n_ % 3]
                bv = buf[:].rearrange("p (j c) -> p j c", j=2)
                S.dma("sp", "stg_" + bn, bv, w_in_v[:, 2 * jp:2 * jp + 2, c0:c0 + 512], writes=[bn])
                wres = ["W1.%d.%d" % (c0 // 128 + q, jp) for q in range(4)]
                if n_ % 2 == 0:
                    tt("dve", W1[:, 2 * jp:2 * jp + 2, c0:c0 + 512], bv, bc(normw_col[:, 2 * jp:2 * jp + 2].unsqueeze(2), [128, 2, 512]), ALU.mult,
                       [bn, "normw_col"], wres)
                else:
                    for q_ in range(2):
                        act(W1[:, 2 * jp + q_, c0:c0 + 512], bv[:, q_, :], AF.Copy, [bn, "normw_col"], wres if q_ == 1 else [],
                            scale=normw_col[:, 2 * jp + q_:2 * jp + q_ + 1])
            return f

        def w_bg():
            bv8 = otmp[:, 0:64].rearrange("p (j c) -> p j c", j=8)
            S.dma("sp", "stg_otmp", bv8, w_in_v[:, :, C_GB:C_GB + 8], writes=["otmp"])
            tt("dve", W1[:, :, C_GB:C_GB + 8], bv8, bc(normw_col[:].unsqueeze(2), [128, 8, 8]), ALU.mult, ["otmp", "normw_col"],
               ["W1.32.%d" % jp for jp in range(4)])

        def mk_w2(j):
            def f():
                stg2 = [(otmp, "otmp"), (o_sb[0], "o_sb0"), (o_sb[1], "o_sb1")]
                buf, bn = stg2[j % 3]
                S.dma("sp", "stg_" + bn, buf[:], w_out[j * 128:(j + 1) * 128, :], writes=[bn])
                g = 0 if j < 4 else 1
                act(W2[:, j, :], buf[:], AF.Copy, [bn, "wn%d" % g], ["W2.%d" % j], scale=wn_col[:, g:g + 1])
            return f
        n_ = 0
        for c0 in [C_GQ, C_GK, C_GV, C_HZ, C_GZ, C_HF, C_HQ, C_HI]:
            for jp in range(4):
                wq.append(mk_w1(n_, c0, jp))
                n_ += 1
        wq.append(w_bg)
        for j in range(8):
            wq.append(mk_w2(j))

        wdone = [0]

        def emit_weights(k):
            for _ in range(min(k, len(wq))):
                wq.pop(0)()
                wdone[0] += 1

        def ensure_weights(n):
            while wdone[0] < n and wq:
                emit_weights(1)

        def w1res(c0, c1):
            return ["W1.%d.%d" % (cb, jp) for cb in range(c0 // 128, (c1 - 1) // 128 + 1) for jp in range(4)]

        S.dma("act", "c_fin", fin_bc[:], fin.partition_broadcast(128), writes=["fin_bc"])
        S.dma("act", "c_alog", alog_bc[:], a_log.partition_broadcast(128), writes=["alog_bc"])
        S.dma("act", "c_dtb", dtb_bc[:], dt_bias.partition_broadcast(128), writes=["dtb_bc"])

        xt = [xr] * 2
        ssq = [sb("ssq%d" % i, [128, 1]) for i in range(2)]
        hbf = [sb("hbf0", [128, D], BF16)] * 2
        hTb = sb("hT", [128, 8, TB], BF16)
        v_tok = sb("v_tok", [128, 4, 512], BF16)
        bg_sb = sb("bg_sb", [128, 4, 8])
        qTt = sb("qTt", [128, 4, TB], BF16); kTt = sb("kTt", [128, 4, TB], BF16)
        ktok = sb("ktok", [128, 4, 512], BF16)
        sTm = sb("sTm", [128, 4, 4, 64], BF16)
        ebl = sb("ebl", [128, 4, 8])
        S_hg = sb("S_hg", [128, 4, 128]); Sp_bf = sb("Sp_bf", [128, 4, 128], BF16)
        S_gd = sb("S_gd", [128, 4, 128]); Sg_bf = sb("Sg_bf", [128, 4, 128], BF16)
        pre = [sb("pre%d" % i, [128, TB + 3]) for i in range(2)]
        cacc = [sb("cacc%d" % i, [128, TB]) for i in range(2)]
        chist = sb("chist", [128, 12, 3])
        kqT = sb("kqT", [128, 4, 2, TB], BF16); kcT = kqT[:, :, 0, :]; qcT = kqT[:, :, 1, :]; vcT = sb("vcT", [128, 4, TB], BF16)
        g_t = sb("g_t", [128, 4, 4]); beta_t = sb("beta_t", [128, 4, 4]); st1 = sb("st1", [128, 4, 4]); st2 = sb("st2", [128, 4, 4])
        lnr = sb("lnr", [128, 4, 8]); rk_t = sb("rk_t", [128, 4, 4])
        GG = sb("GG", [128, 4, 8]); gm = sb("gm", [128, 4, 2, 4]); eGl = sb("eGl", [128, 4, 2, 4])
        f_rhsk = sb("f_rhsk", [128, 4, 4]); f_dec = sb("f_dec", [128, 4, 4]); nbr = sb("nbr", [128, 4, 4]); f_o = sb("f_o", [128, 4, 4])
        rhs_vk = [sb("rhs_vk%d" % i, [128, 4, 256], BF16) for i in range(2)]
        kdec = [sb("kdec%d" % i, [128, 512], BF16) for i in range(2)]
        R1 = sb("R1", [128, 4, 64]); R2 = sb("R2", [128, 4, 64]); R3 = sb("R3", [128, 4, 64]); R4 = sb("R4", [128, 4, 64])
        LL = sb("LL", [128, 512]); tmpA = sb("tmpA", [128, 4, 64]); tmpB = sb("tmpB", [128, 4, 64])
        X0 = sb("X0", [128, 4, 64], BF16)
        attnT = [sb("attnT%d" % i, [128, 4, 64], BF16) for i in range(2)]
        XZ = [sb("XZ%d" % i, [128, 2, 4, 64], BF16) for i in range(2)]
        Qb = [sb("Qb%d" % i, [128, 4, 64], BF16) for i in range(2)]
        Qfin = [sb("Qfin%d" % i, [128, 4, 64], BF16) for i in range(2)]
        nWkT = [sb("nWkT%d" % i, [128, 4, 128], BF16) for i in range(2)]
        u_bf = sb("u_bf", [128, 4, 128], BF16)
        sqb = [u_bf[:].rearrange("p h v -> p (h v)")] * 2
        sz = sb("sz", [128, 4, D], BF16)
        ssq8 = sb("ssq8", [128, 8])
        ybf = sb("ybf", [128, D], BF16); yT = sb("yT", [128, 8, 128], BF16)
        ssqf = sb("ssqf", [128, 1])

        H4 = lambda n: ["%s.%d" % (n, h_) for h_ in range(4)]
        S.op(P, "memset", S_hg[:], 0.0, writes=["S_hg"] + H4("S_hg"))
        S.op(P, "memset", S_gd[:], 0.0, writes=["S_gd"] + H4("S_gd"))
        S.op(P, "memset", Sg_bf[:], 0.0, writes=["Sg_bf"] + H4("Sg_bf"))
        S.op(P, "memset", chist[:], 0.0, writes=["chist"])

        def rstd_from_ssq(col, rows, n, r):
            act(col, col, AF.Ln, [r, "eps_c"], [r], bias=eps_c[0:rows, :], scale=1.0 / n)
            act(col, col, AF.Exp, [r], [r], scale=-0.5)

        def fm_proj(c0, evac):
            pa, pan = next_ps("acc")
            for j in range(8):
                mm(pa[:, :], W1[:, j, c0:c0 + 128], hTb[:, j, :], j == 0, j == 7, ["hT"] + w1res(c0, c0 + 128), [pan], inc=(j == 7))
            evac(pa, pan)

        def tm_proj(i, c0, ncols, evac):
            pa, pan = next_ps("acc")
            for j in range(8):
                mm(pa[:, 0:ncols], hTb[:, j, i * 128:(i + 1) * 128], W1[:, j, c0:c0 + ncols], j == 0, j == 7,
                   ["hT"] + w1res(c0, c0 + ncols), [pan], inc=(j == 7))
            evac(pa, pan)

        def emit_derived_consts():
            tt("dve", lbt[:], lbl_sb[:, 1, :], lbl_sb[:, 0, :], ALU.subtract, ["lbl_sb"], ["lbt"])
            act(lb_c[:], lbt[:], AF.Exp, ["lbt"], ["lb_c"])
            act(lb_c[:], lb_c[:], AF.Ln, ["lb_c"], ["lb_c"], bias=1.0)
            act(lb_c[:], lb_c[:], AF.Exp, ["lb_c"], ["lb_c"], scale=-1.0)
            act(lnoml_c[:], lbt[:], AF.Exp, ["lbt"], ["lnoml_c"], scale=-1.0)
            act(lnoml_c[:], lnoml_c[:], AF.Ln, ["lnoml_c"], ["lnoml_c"], bias=1.0)
            ts("dve", lnoml_c[:], lnoml_c[:], -1.0, None, ALU.mult, None, ["lnoml_c"], ["lnoml_c"])
            act(negA_bc[:], alog_bc[:], AF.Exp, ["alog_bc"], ["negA_bc"])
            ts("dve", negA_bc[:], negA_bc[:], -1.0, None, ALU.mult, None, ["negA_bc"], ["negA_bc"])


        tcount = [0]
        pcount = [0]
        dbgq = []

        def phase_x(blk, i, xbuf, xres, xsem):
            tok0 = blk * TB + i * 128
            slot = tcount[0] % 2
            tcount[0] += 1
            sn = "ssq%d" % slot; hn = "hbf0"
            pre_ = blk >= 1
            xq = "sp" if pre_ else "act"
            if i == 0:
                S.dma(xq, xsem, xbuf[:], xp[tok0:tok0 + 128, :], writes=xres)
            act(hbf[0][:], xbuf[:], AF.Square, xres, [hn, sn], accum_out=ssq[slot][:])
            rstd_from_ssq(ssq[slot][:], 128, D, sn)
            ts("dve", hbf[slot][:], xbuf[:], ssq[slot][:], None, ALU.mult, None, xres + [sn], [hn])
            if i < 3:
                S.dma(xq, xsem, xbuf[:], xp[tok0 + 128:tok0 + 256, :], writes=xres)
            pt, ptn = next_ps("trp")
            for j in range(8):
                tr(pt[:, j * 128:(j + 1) * 128], hbf[slot][:, j * 128:(j + 1) * 128], ident_bf[:], [hn, "ident_bf"], [ptn], inc=(j == 7))
            cp("act" if i % 2 == 0 else "dve", hTb[:, :, i * 128:(i + 1) * 128], pt[:].rearrange("p (j t) -> p j t", j=8), [ptn], ["hT"])


        for blk in range(NBLK):
            if blk == 0:
                for i in range(4):
                    phase_x(0, i, xr, ["xr"], "xr")
                    emit_weights(3)
            deferred = []
            deferred2 = []
            for kind, cbase, dst, dn in ((0, C_GQ, qcT, "qcT"), (1, C_GK, kcT, "kcT"), (2, C_GV, vcT, "vcT")):
                for h in range(4):
                    ct = kind * 4 + h
                    if blk == 0:
                        ensure_weights(4 * (kind + 1))
                        emit_weights(3)
                    pslot = pcount[0] % 2
                    pcount[0] += 1
                    pb = pre[pslot]; pn = "pre%d" % pslot
                    ca = cacc[pcount[0] % 2]; cn = "cacc%d" % (pcount[0] % 2)

                    def ev_c(pa, pan, pb=pb, pn=pn, ca=ca, cn=cn, ct=ct):
                        cp("act", pb[:, 3:TB + 3], pa[:, :], [pan], [pn])
                        act(ca[:], pa[:, :], AF.Copy, [pan, "cw_c"], [cn], scale=cw_c[:, ct, 3:4])
                    cp(P, pb[:, 0:3], chist[:, ct, :], ["chist", "chist.%d" % ct], [pn])
                    fm_proj(cbase + h * 128, ev_c)
                    stt(ca[:], pb[:, 2:TB + 2], cw_c[:, ct, 2:3], ca[:], ALU.mult, ALU.add, [pn, cn, "cw_c"], [cn])
                    stt(ca[:], pb[:, 1:TB + 1], cw_c[:, ct, 1:2], ca[:], ALU.mult, ALU.add, [pn, cn], [cn])
                    stt(ca[:], pb[:, 0:TB], cw_c[:, ct, 0:1], ca[:], ALU.mult, ALU.add, [pn, cn], [cn])
                    cp(P, chist[:, ct, :], pb[:, TB:TB + 3], [pn], ["chist.%d" % ct])
                    if blk == NBLK - 1:
                        S.dma("sp", "o_cv%d" % pslot, o_cv[:, ct * 128:(ct + 1) * 128].rearrange("j c -> c j"), pb[:, TB:TB + 3], reads=[pn],
                              allow_slow_non_contiguous=True)
                        outs_done.append(pn)
                    def post(kind=kind, h=h, dst=dst, dn=dn, ca=ca, cn=cn):
                        act(dst[:, h, :], ca[:], AF.Silu, [cn], ["%s.%d" % (dn, h)])
                        if kind < 2:
                            sq = sqb[0]; sqn = "u_bf"
                            tt(P, sq, dst[:, h, :], dst[:, h, :], ALU.mult, ["%s.%d" % (dn, h)], [sqn])

                            def post2():
                                for i in range(4):
                                    mm(misc[:, 64 + i * 8 + kind * 4 + h: 64 + i * 8 + kind * 4 + h + 1], sq[:, i * 128:(i + 1) * 128], ones_bf[:, 0:1], True, True,
                                       [sqn, "ones_bf"], ["rcA"], inc=(i == 3))
                            deferred2.append(post2)
                    while len(deferred2) > 0:
                        deferred2.pop(0)()
                    deferred.append(post)
                    while len(deferred) > 1:
                        deferred.pop(0)()
            while deferred:
                deferred.pop(0)()
            while deferred2:
                deferred2.pop(0)()
            if blk == 0:
                emit_weights(1000)
            for i in range(4):
                for half, cbase in ((0, C_HZ), (1, C_GZ)):
                    def ev_z(pa, pan, i=i, half=half):
                        act(sz[:, i, half * 512:(half + 1) * 512], pa[:, :], AF.Silu, [pan], ["sz.%d.%d" % (i, half)])
                    tm_proj(i, cbase, 512, ev_z)

            if blk == 0:
                emit_derived_consts()
            setA = dict(e=e_sb, l1=l1, l2=l2, bc=bcum, dd=dd, n=dict(e="e_sb", l1="l1", l2="l2", bc="bcum", dd="dd"))
            setB = dict(e=o_sb[0][:, 0:TB], l1=o_sb[0][:, TB:2 * TB], l2=o_sb[1][:, 0:TB], bc=o_sb[1][:, TB:2 * TB], dd=otmp[:, 0:TB],
                        n=dict(e="o_sb0", l1="o_sb0", l2="o_sb1", bc="o_sb1", dd="otmp"))

            def chain_ops(h, st_):
                e_, l1_, l2_, bc_, dd_ = st_["e"], st_["l1"], st_["l2"], st_["bc"], st_["dd"]
                n = st_["n"]
                b3 = bc_.rearrange("p (c t) -> p c t", t=64)
                ops = []

                def ev_hf(pa, pan):
                    act(e_, pa[:, :], AF.Exp, [pan], [n["e"]], scale=-1.0)
                ops.append(lambda: fm_proj(C_HF + h * 128, ev_hf))
                ops.append(lambda: act(l1_, e_, AF.Ln, [n["e"], "lb_c"], [n["l1"]], scale=lb_c[:, h:h + 1], bias=1.0))
                ops.append(lambda: act(l2_, e_, AF.Ln, [n["e"]], [n["l2"]], bias=1.0))
                ops.append(lambda: tt(P, l1_, l1_, l2_, ALU.subtract, [n["l1"], n["l2"]], [n["l1"]]))
                ops.append(lambda: S.op("dve", "tensor_tensor_scan", bc_, cmask[:], l1_, 0.0, ALU.mult, ALU.add, reads=["cmask", n["l1"]], writes=[n["bc"]]))
                ops.append(lambda: tt(P, dd_.rearrange("p (c t) -> p c t", t=64), b3, bc(b3[:, :, 63:64], [128, 8, 64]), ALU.subtract, [n["bc"]], [n["dd"]]))
                ops.append(lambda: act(ebl[:, h, :], b3[:, :, 63], AF.Exp, [n["bc"]], ["ebl.%d" % h]))
                ops.append(lambda: tt(P, l2_, l2_, dd_, ALU.add, [n["l2"], n["dd"]], [n["l2"]]))
                ops.append(lambda: act(dd_, dd_, AF.Exp, [n["dd"]], [n["dd"]]))
                ops.append(lambda: act(l2_, l2_, AF.Exp, [n["l2"], "lnoml_c"], [n["l2"]], scale=-1.0, bias=lnoml_c[:, h:h + 1]))
                ops.append(lambda: tt("dve", kTt[:, h, :], e_, l2_, ALU.mult, [n["e"], n["l2"]], ["kTt.%d" % h]))

                def ev_hq(pa, pan):
                    tt("dve", qTt[:, h, :], pa[:, :], dd_, ALU.mult, [pan, n["dd"]], ["qTt.%d" % h])
                ops.append(lambda: fm_proj(C_HQ + h * 128, ev_hq))
                return ops

            def tm_ops(i):
                def ev_v(pa, pan):
                    cp("dve", v_tok[:, i, :], pa[:, :], [pan], ["v_tok.%d" % i])

                def ev_bg(pa, pan):
                    cp("act", bg_sb[:, i, :], pa[:, 0:8], [pan], ["bg_sb"])
                return [lambda: tm_proj(i, C_HI, 512, ev_v), lambda: tm_proj(i, C_GB, 8, ev_bg)]

            for pair in range(2):
                ca_, cb_ = chain_ops(2 * pair, setA), chain_ops(2 * pair + 1, setB)
                extra = tm_ops(2 * pair) + tm_ops(2 * pair + 1)
                for k_ in range(len(ca_)):
                    ca_[k_]()
                    cb_[k_]()
                    if k_ in (3, 5, 7, 9) and extra:
                        extra.pop(0)()
                while extra:
                    extra.pop(0)()

            tt(P, st1[:], bg_sb[:, :, 4:8], bc(dtb_bc[:].unsqueeze(1), [128, 4, 4]), ALU.add, ["bg_sb", "dtb_bc"], ["st1"])
            act(st1[:], st1[:], AF.Exp, ["st1"], ["st1"])
            act(st1[:], st1[:], AF.Ln, ["st1"], ["st1"], bias=1.0)
            tt(P, g_t[:], st1[:], bc(negA_bc[:].unsqueeze(1), [128, 4, 4]), ALU.mult, ["st1", "negA_bc"], ["g_t"])
            act(st2[:], bg_sb[:, :, 0:4], AF.Exp, ["bg_sb"], ["st2"], scale=-1.0)
            act(st2[:], st2[:], AF.Ln, ["st2"], ["st2"], bias=1.0)
            act(beta_t[:], st2[:], AF.Exp, ["st2"], ["beta_t"], scale=-1.0)
            act(lnr[:], misc[:, 64:96].rearrange("p (i e) -> p i e", e=8), AF.Ln, ["rcA", "eps_c"], ["lnr"], bias=eps_c[:, :])
            ts("dve", lnr[:], lnr[:], -0.5, None, ALU.mult, None, ["lnr"], ["lnr"])
            ts("dve", lnr[:, :, 0:4], lnr[:, :, 0:4], lnk_c[:, 0:1], None, ALU.add, None, ["lnr", "lnk_c"], ["lnr"])
            act(rk_t[:], lnr[:, :, 4:8], AF.Exp, ["lnr"], ["rk_t"])
            ts("dve", gm[:, :, 0, :], g_t[:], hm[:, 0:1], None, ALU.mult, None, ["g_t", "hm"], ["gm"])
            ts("dve", gm[:, :, 1, :], g_t[:], hm[:, 1:2], None, ALU.mult, None, ["g_t", "hm", "gm"], ["gm"])
            pm, pmn = misc, "rcA"
            for i in range(4):
                mm(pm[:, i * 8:i * 8 + 4], Tri_bd[:], g_t[:, i, :], True, True, ["Tri_bd", "g_t"], [pmn], inc=False)
                mm(pm[:, i * 8 + 4:i * 8 + 8], BD[:], g_t[:, i, :], True, True, ["BD", "g_t"], [pmn], inc=False)
            mm(pm[:, 32:64], ones_f[:], gm[:].rearrange("p i c h -> p (i c h)"), True, True, ["ones_f", "gm"], [pmn])
            cp("dve", GG[:], pm[:, 0:32].rearrange("p (i e) -> p i e", e=8), [pmn], ["GG"])
            act(eGl[:].rearrange("p i c h -> p (i c h)"), pm[:, 32:64], AF.Exp, [pmn], ["eGl"])
            tt(P, st1[:], GG[:, :, 0:4], lnr[:, :, 4:8], ALU.add, ["GG", "lnr"], ["st1"])
            act(f_rhsk[:], st1[:], AF.Exp, ["st1"], ["f_rhsk"])
            tt(P, f_rhsk[:], f_rhsk[:], beta_t[:], ALU.mult, ["f_rhsk", "beta_t"], ["f_rhsk"])
            tt(P, st2[:], GG[:, :, 4:8], st1[:], ALU.subtract, ["GG", "st1"], ["st2"])
            tt(P, st1[:], lnr[:, :, 4:8], lnr[:, :, 4:8], ALU.add, ["lnr", "st2"], ["st1"])
            tt(P, st2[:], st2[:], st1[:], ALU.add, ["st1", "st2"], ["st2"])
            act(f_dec[:], st2[:], AF.Exp, ["st2"], ["f_dec"])
            tt(P, nbr[:], beta_t[:], rk_t[:], ALU.mult, ["beta_t", "rk_t"], ["nbr"])
            ts("dve", nbr[:], nbr[:], -1.0, None, ALU.mult, None, ["nbr"], ["nbr"])
            tt(P, st1[:], GG[:, :, 0:4], lnr[:, :, 0:4], ALU.add, ["GG", "lnr", "f_rhsk"], ["st1"])
            act(f_o[:], st1[:], AF.Exp, ["st1"], ["f_o"])

            dump("GG", GG[:].rearrange("p i e -> p (i e)"), [128, 32], ["GG"])
            dump("eGl", eGl[:].rearrange("p i c h -> p (i c h)"), [128, 32], ["eGl"])
            dump("f_dec", f_dec[:].rearrange("p i e -> p (i e)"), [128, 16], ["f_dec"])
            dump("f_o", f_o[:].rearrange("p i e -> p (i e)"), [128, 16], ["f_o"])
            dump("f_rhsk", f_rhsk[:].rearrange("p i e -> p (i e)"), [128, 16], ["f_rhsk"])
            dump("g_t", g_t[:].rearrange("p i e -> p (i e)"), [128, 16], ["g_t"])
            dump("beta_t", beta_t[:].rearrange("p i e -> p (i e)"), [128, 16], ["beta_t"])
            dump("lnr", lnr[:].rearrange("p i e -> p (i e)"), [128, 32], ["lnr"])
            def tile_stages(i):
                tok0 = blk * TB + i * 128
                tsl = slice(i * 128, (i + 1) * 128)
                par = i % 2
                rv = rhs_vk[par]; rvn = "rhs_vk%d" % par
                kd = kdec[par]; kdn = "kdec%d" % par
                at = attnT[par]; atn = "attnT%d" % par
                nw = nWkT[par]; nwn = "nWkT%d" % par
                qf = Qfin[par]; qfn = "Qfin%d" % par
                osb = o_sb[par]; osn = "o_sb%d" % par
                prep, post = [], []

                def blk_mm(out_fn, l_fn, r_fn, reads, pn_, last):
                    for h in range(4):
                        for c2 in range(2):
                            pr = slice(c2 * 64, c2 * 64 + 64)
                            mm(out_fn(pr, h), l_fn(pr, h), r_fn(pr, h), True, True, reads, [pn_], inc=(last and h == 3 and c2 == 1))

                def p_hgrn():
                    pt, ptn = next_ps("trp")
                    for h in range(4):
                        tr(pt[:, h * 128:(h + 1) * 128], kTt[:, h, tsl], ident_bf[:], ["kTt.%d" % h, "ident_bf"], [ptn], inc=(h == 3))
                    cp("act", ktok[:, i, :], pt[:, 0:512], [ptn], ["ktok.%d" % i])
                    pm, pmn = next_ps("pp")
                    for h in range(4):
                        for c2 in range(2):
                            pr = slice(c2 * 64, c2 * 64 + 64)
                            cs = slice(i * 128 + c2 * 64, i * 128 + c2 * 64 + 64)
                            mm(pm[pr, h * 64:(h + 1) * 64], kTt[:, h, cs], qTt[:, h, cs], True, True, ["kTt.%d" % h, "qTt.%d" % h], [pmn],
                               inc=(h == 3 and c2 == 1))
                    tt("dve", sTm[:, i, :, :], pm[:, 0:256].rearrange("p (h t) -> p h t", h=4), bc(TriLoc[:].unsqueeze(1), [128, 4, 64]), ALU.mult,
                       [pmn, "TriLoc"], ["sTm.%d" % i])
                prep.append(p_hgrn)

                def p_tok():
                    pt, ptn = next_ps("trp")
                    for h in range(4):
                        tr(pt[:, h * 128:(h + 1) * 128], kcT[:, h, tsl], ident_bf[:], ["kcT.%d" % h, "ident_bf"], [ptn], inc=False)
                    for h in range(4):
                        tr(pt[:, 512 + h * 128:512 + (h + 1) * 128], vcT[:, h, tsl], ident_bf[:], ["vcT.%d" % h, "ident_bf"], [ptn], inc=(h == 3))
                    tt("dve", rv[:, :, 0:128], pt[:, 512:1024].rearrange("p (h v) -> p h v", h=4), bc(beta_t[:, i, :].unsqueeze(2), [128, 4, 128]), ALU.mult,
                       [ptn, "beta_t"], [rvn])
                    tt("dve", rv[:, :, 128:256], pt[:, 0:512].rearrange("p (h v) -> p h v", h=4), bc(f_rhsk[:, i, :].unsqueeze(2), [128, 4, 128]), ALU.mult,
                       [ptn, "f_rhsk", rvn], [rvn])
                    tt("dve", kd[:].rearrange("p (h v) -> p h v", h=4), pt[:, 0:512].rearrange("p (h v) -> p h v", h=4),
                       bc(f_dec[:, i, :].unsqueeze(2), [128, 4, 128]), ALU.mult, [ptn, "f_dec"], [kdn])
                prep.append(p_tok)

                def p_mats():
                    pk, pkn = next_ps("pp")
                    for h in range(4):
                        for c2 in range(2):
                            pr = slice(c2 * 64, c2 * 64 + 64)
                            cs = slice(i * 128 + c2 * 64, i * 128 + c2 * 64 + 64)
                            mm(pk[pr, h * 128:(h + 1) * 128], kcT[:, h, cs], kqT[:, h, :, cs], True, True, ["kcT.%d" % h, "qcT.%d" % h], [pkn],
                               inc=(h == 3 and c2 == 1))
                    pk4 = pk[:, :].rearrange("p (h a s) -> p h a s", h=4, a=2)
                    gi = bc(g_t[:, i, :].unsqueeze(2), [128, 4, 64])
                    tt(P, R1[:], bc(SUloc[:].unsqueeze(1), [128, 4, 64]), gi, ALU.mult, ["SUloc", "g_t"], ["R1"])
                    tt(P, R2[:], bc(Iloc[:].unsqueeze(1), [128, 4, 64]), bc(lnr[:, i, 4:8].unsqueeze(2), [128, 4, 64]), ALU.mult, ["Iloc", "lnr"], ["R2"])
                    tt(P, R3[:], bc(TriLoc[:].unsqueeze(1), [128, 4, 64]), gi, ALU.mult, ["TriLoc", "g_t"], ["R3"])
                    tt(P, R4[:], bc(Iloc[:].unsqueeze(1), [128, 4, 64]), bc(lnr[:, i, 0:4].unsqueeze(2), [128, 4, 64]), ALU.mult, ["Iloc", "lnr"], ["R4"])
                    pd, pdn = next_ps("pp")
                    fl = lambda t_: t_[:].rearrange("p h s -> p (h s)")
                    mm(pd[:, 0:256], ident_f[:], fl(NEGs4), True, False, ["ident_f", "NEGs4"], [pdn], inc=False)
                    mm(pd[:, 0:256], Tri_bd[:], fl(R1), False, False, ["Tri_bd", "R1"], [pdn], inc=False)
                    mm(pd[:, 0:256], BD[:], fl(R2), False, True, ["BD", "R2"], [pdn], inc=False)
                    mm(pd[:, 256:512], ident_f[:], fl(NEGTi4), True, False, ["ident_f", "NEGTi4"], [pdn], inc=False)
                    mm(pd[:, 256:512], SU_bd[:], fl(R3), False, False, ["SU_bd", "R3"], [pdn], inc=False)
                    mm(pd[:, 256:512], BD[:], fl(R4), False, True, ["BD", "R4"], [pdn], inc=True)
                    act(LL[:], pd[:, :], AF.Exp, [pdn], ["LL"])
                    tt("dve", tmpA[:], pk4[:, :, 0, :], LL[:, 0:256].rearrange("p (h s) -> p h s", h=4), ALU.mult,
                       [pkn, "LL"], ["tmpA"])
                    tt(P, X0[:], tmpA[:], bc(nbr[:, i, :].unsqueeze(2), [128, 4, 64]), ALU.mult, ["tmpA", "nbr"], ["X0"])
                    tt("dve", tmpB[:], pk4[:, :, 1, :], LL[:, 256:512].rearrange("p (h s) -> p h s", h=4), ALU.mult,
                       [pkn, "LL"], ["tmpB"])
                    tt(P, at[:], tmpB[:], bc(rk_t[:, i, :].unsqueeze(2), [128, 4, 64]), ALU.mult, ["tmpB", "rk_t"], [atn])
                prep.append(p_mats)

                st = {}

                def p_z0():
                    pz, pzn = next_ps("pp")
                    blk_mm(lambda pr, h: pz[pr, h * 64:(h + 1) * 64], lambda pr, h: X0[pr, h, :], lambda pr, h: ident_bf[pr, pr], ["X0", "ident_bf"], pzn, True)
                    xz = XZ[0]; xzn = "XZ0"
                    cp(P, xz[:, 0, :, :], X0[:], ["X0"], [xzn])
                    cp("act", xz[:, 1, :, :], pz[:, 0:256].rearrange("p (h s) -> p h s", h=4), [pzn, xzn], [xzn])
                    tt("dve", Qb[0][:], pz[:, 0:256].rearrange("p (h s) -> p h s", h=4), bc(Iloc[:].unsqueeze(1), [128, 4, 64]), ALU.add, [pzn, "Iloc"], ["Qb0"])
                    st["q"] = (Qb[0], "Qb0")
                prep.append(p_z0)

                def mk_level(lev):
                    def p_lev():
                        qcur, qn = st["q"]
                        xz = XZ[lev % 2]; xzn = "XZ%d" % (lev % 2)
                        xz2 = XZ[(lev + 1) % 2]; xz2n = "XZ%d" % ((lev + 1) % 2)
                        px, pxn = next_ps("pp")
                        blk_mm(lambda pr, h: px[pr, h * 64:(h + 1) * 64], lambda pr, h: xz[pr, 1, h, :], lambda pr, h: xz[pr, 0, h, :], [xzn], pxn, False)
                        blk_mm(lambda pr, h: px[pr, 256 + h * 64:256 + (h + 1) * 64], lambda pr, h: xz[pr, 0, h, :], lambda pr, h: xz[pr, 1, h, :], [xzn], pxn, True)
                        cp("act", xz2[:].rearrange("p a h s -> p (a h s)"), px[:, :], [pxn], [xz2n])
                        pq, pqn = next_ps("pp")
                        blk_mm(lambda pr, h: pq[pr, h * 64:(h + 1) * 64], lambda pr, h: xz2[pr, 0, h, :], lambda pr, h: qcur[pr, h, :], [xz2n, qn], pqn, True)
                        if lev < 4:
                            qnx, qnn = Qb[(lev + 1) % 2], "Qb%d" % ((lev + 1) % 2)
                        else:
                            qnx, qnn = qf, qfn
                        tt("dve", qnx[:], pq[:, 0:256].rearrange("p (h s) -> p h s", h=4), qcur[:], ALU.add, [pqn, qn], [qnn])
                        st["q"] = (qnx, qnn)
                    return p_lev
                for lev in range(5):
                    prep.append(mk_level(lev))

                def p_wk():
                    for c2 in range(2):
                        pr = slice(c2 * 64, c2 * 64 + 64)
                        pw, pwn = next_ps("pp")
                        for h in range(4):
                            mm(pw[:, h * 64:(h + 1) * 64], rv[pr, h, 128:256], qf[pr, h, :], True, True, [rvn, qfn], [pwn], inc=(h == 3))
                        act(nw[:, :, c2 * 64:(c2 + 1) * 64], pw[:, 0:256].rearrange("p (h t) -> p h t", h=4), AF.Copy, [pwn, nwn], [nwn], scale=-1.0)
                prep.append(p_wk)

                def mk_hg(c2):
                    def r_hg():
                        po, pon = rcA, "rcA"
                        c = i * 2 + c2
                        pr = slice(c2 * 64, c2 * 64 + 64)
                        cs = slice(i * 128 + c2 * 64, i * 128 + c2 * 64 + 64)
                        for h in range(4):
                            ts("dve", Sp_bf[:, h, :], S_hg[:, h, :], ebl[:, h, c:c + 1], None, ALU.mult, None, ["S_hg.%d" % h, "ebl.%d" % h], ["Sp_bf.%d" % h])
                        psb, psbn = acc[0], "acc0"
                        for h in range(4):
                            hv = slice(h * 128, (h + 1) * 128)
                            mm(po[pr, hv], sTm[pr, i, h, :], v_tok[pr, i, hv], True, False, ["sTm.%d" % i, "v_tok.%d" % i], [pon], inc=False)
                            mm(po[pr, hv], qTt[:, h, cs], Sp_bf[:, h, :], False, True, ["qTt.%d" % h, "Sp_bf.%d" % h], [pon], inc=False)
                            mm(psb[:, hv], ktok[pr, i, hv], v_tok[pr, i, hv], True, True, ["ktok.%d" % i, "v_tok.%d" % i], [psbn], inc=(h == 3))
                        for h in range(4):
                            hv = slice(h * 128, (h + 1) * 128)
                            stt(S_hg[:, h, :], S_hg[:, h, :], ebl[:, h, c:c + 1], psb[:, hv], ALU.mult, ALU.add, ["S_hg.%d" % h, "ebl.%d" % h, psbn], ["S_hg.%d" % h])
                        if c2 == 1:
                            cp("act", osb[:, 0:512], po[:, :], [pon], [osn])
                    return r_hg
                post.append(mk_hg(0)); post.append(mk_hg(1))

                def mk_gd(c2):
                    def r_gd():
                        pu, pun = rcA, "rcA"
                        poa, poan = rcB, "rcB"
                        pob, pobn = acc[1], "acc1"
                        pss, pssn = acc[0], "acc0"
                        pr = slice(c2 * 64, c2 * 64 + 64)
                        cs = slice(i * 128 + c2 * 64, i * 128 + c2 * 64 + 64)
                        for h in range(4):
                            hv = slice(h * 128, (h + 1) * 128)
                            mm(pu[pr, hv], qf[pr, h, :], rv[pr, h, 0:128], True, False, [qfn, rvn], [pun], inc=False)
                            mm(pu[pr, hv], nw[:, h, c2 * 64:(c2 + 1) * 64], Sg_bf[:, h, :], False, True, [nwn, "Sg_bf.%d" % h], [pun], inc=False)
                            mm(pob[pr, hv], qcT[:, h, cs], Sg_bf[:, h, :], True, True, ["qcT.%d" % h, "Sg_bf.%d" % h], [pobn], inc=(h == 3))
                        cp("act", u_bf[pr, :, :], pu[pr, :].rearrange("p (h v) -> p h v", h=4), [pun, "u_bf"], ["u_bf"])
                        for h in range(4):
                            hv = slice(h * 128, (h + 1) * 128)
                            mm(poa[pr, hv], at[pr, h, :], u_bf[pr, h, :], True, True, [atn, "u_bf"], [poan], inc=False)
                            mm(pss[:, hv], kd[pr, hv], u_bf[pr, h, :], True, True, [kdn, "u_bf"], [pssn], inc=(h == 3))
                        for h in range(4):
                            hv = slice(h * 128, (h + 1) * 128)
                            stt(S_gd[:, h, :], S_gd[:, h, :], eGl[:, i, c2, h:h + 1], pss[:, hv], ALU.mult, ALU.add, ["S_gd.%d" % h, "eGl", pssn], ["S_gd.%d" % h])
                            cp("dve", Sg_bf[:, h, :], S_gd[:, h, :], ["S_gd.%d" % h], ["Sg_bf.%d" % h])
                        if c2 == 1:
                            cp("act", otmp[:, 0:512], poa[:, :], [poan], ["otmp"])
                            for h in range(4):
                                hv = slice(h * 128, (h + 1) * 128)
                                stt(osb[:, 512 + h * 128:512 + (h + 1) * 128], pob[:, hv], f_o[:, i, h:h + 1], otmp[:, hv], ALU.mult, ALU.add,
                                    [pobn, "f_o", "otmp", osn], [osn])
                            if "mix" in dbg:
                                S.dma("sp", "dbg%d" % par, dbg_out["d_o"][tok0:tok0 + 128, :], osb[:], reads=[osn])
                                S.wait_all("sp", [osn])
                    return r_gd
                post.append(mk_gd(0)); post.append(mk_gd(1))

                def o_norm():
                    S.dma("sp", "xr", xr[:], xp[tok0:tok0 + 128, :], writes=["xr"])
                    for hh in range(8):
                        act(ybf[:, 0:128], osb[:, hh * 128:(hh + 1) * 128], AF.Square, [osn], ["ybf", "ssq8"], accum_out=ssq8[:, hh:hh + 1])
                    rstd_from_ssq(ssq8[:], 128, 128, "ssq8")
                    for hh in range(8):
                        stt(ybf[:, hh * 128:(hh + 1) * 128], osb[:, hh * 128:(hh + 1) * 128], ssq8[:, hh:hh + 1], sz[:, i, hh * 128:(hh + 1) * 128],
                            ALU.mult, ALU.mult, [osn, "ssq8", "sz.%d.0" % i, "sz.%d.1" % i, "ybf"], ["ybf"])
                post.append(o_norm)

                def o_proj():
                    pt, ptn = next_ps("trp")
                    for j in range(8):
                        tr(pt[:, j * 128:(j + 1) * 128], ybf[:, j * 128:(j + 1) * 128], ident_bf[:], ["ybf", "ident_bf"], [ptn], inc=(j == 7))
                    cp("dve", yT[:].rearrange("p j t -> p (j t)"), pt[:, :], [ptn], ["yT"])
                    for half in range(2):
                        pa, pan = next_ps("pp")
                        for j in range(8):
                            mm(pa[:, :], yT[:, j, :], W2[:, j, half * 512:(half + 1) * 512], j == 0, j == 7, ["yT", "W2.%d" % j], [pan], inc=(j == 7))
                        tt("dve", xr[:, half * 512:(half + 1) * 512], pa[:, :], xr[:, half * 512:(half + 1) * 512], ALU.add, [pan, "xr"], ["xr"])
                post.append(o_proj)

                def o_fin():
                    act(ybf[:], xr[:], AF.Square, ["xr"], ["ybf", "ssqf"], accum_out=ssqf[:])
                    rstd_from_ssq(ssqf[:], 128, D, "ssqf")
                    stt(xr[:], xr[:], ssqf[:], fin_bc[:], ALU.mult, ALU.mult, ["xr", "ssqf", "fin_bc"], ["xr"])
                    S.dma("sp", "yo", yp[tok0:tok0 + 128, :], xr[:], reads=["xr"])
                post.append(o_fin)
                return prep, post

            stages = [tile_stages(i) for i in range(4)]
            for f in stages[0][0]:
                f()
            for i in range(5):
                lists = []
                if i + 1 <= 3:
                    lists.append(stages[i + 1][0])
                if i <= 3:
                    lists.append(stages[i][1][0:4])
                if i >= 1:
                    lists.append(stages[i - 1][1][4:])
                if blk + 1 < NBLK and i < 4:
                    lists.append([(lambda i=i: phase_x(blk + 1, i, xt2, ["e_sb", "l1"], "xt2"))])
                pos = [0] * len(lists)
                total = sum(len(l) for l in lists)
                for _ in range(total):
                    best = None
                    for k_, l in enumerate(lists):
                        if pos[k_] < len(l):
                            frac = pos[k_] / float(len(l))
                            if best is None or frac < best[0]:
                                best = (frac, k_)
                    k_ = best[1]
                    lists[k_][pos[k_]]()
                    pos[k_] += 1
        S.dma("sp", "o_shg", o_shg.rearrange("h k v -> k h v"), S_hg[:], reads=["S_hg"] + H4("S_hg"))
        S.dma("sp", "o_sgd", o_sgd.rearrange("h k v -> k h v"), S_gd[:], reads=["S_gd"] + H4("S_gd"))
        outs_done += ["S_hg", "S_gd", "xr"]


        NB_ = NS
        S.dma("sp", "xr", xr[0:NB_, :], xs, writes=["xr"])
        act(hbf[0][0:NB_, :], xr[0:NB_, :], AF.Square, ["xr"], ["hbf0", "ssq0"], accum_out=ssq[0][0:NB_, :])
        rstd_from_ssq(ssq[0][0:NB_, :], NB_, D, "ssq0")
        ts("dve", hbf[0][0:NB_, :], xr[0:NB_, :], ssq[0][0:NB_, :], None, ALU.mult, None, ["xr", "ssq0"], ["hbf0"])
        pt, ptn = next_ps("trp")
        for j in range(8):
            tr(pt[:, j * NB_:(j + 1) * NB_], hbf[0][0:NB_, j * 128:(j + 1) * 128], ident_bf[0:NB_, 0:NB_], ["hbf0", "ident_bf"], [ptn], inc=(j == 7))
        hsT = hTb[:, :, 0:NB_]
        cp("act", hsT, pt[:, 0:8 * NB_].rearrange("p (j t) -> p j t", j=8), [ptn], ["hT"])
        PT = e_sb[:].rearrange("p (t b) -> p t b", b=NB_)
        pa, pan = next_ps("acc")
        for ctl in range(32):
            for j in range(8):
                mm(pa[:, ctl * NB_:(ctl + 1) * NB_], W1[:, j, ctl * 128:(ctl + 1) * 128], hsT[:, j, :], j == 0, j == 7,
                   ["hT"] + w1res(ctl * 128, ctl * 128 + 128), [pan], inc=(ctl == 31 and j == 7))
        cp("dve", e_sb[:], pa[:, :], [pan], ["e_sb"])
        bgT = l1[0:8, 0:NB_]
        pa2, pa2n = next_ps("acc")
        for j in range(8):
            mm(pa2[0:8, 0:NB_], W1[:, j, C_GB:C_GB + 8], hsT[:, j, :], j == 0, j == 7, ["hT"] + w1res(C_GB, C_GB + 8), [pa2n], inc=(j == 7))
        cp("act", bgT, pa2[0:8, 0:NB_], [pa2n], ["l1"])
        for n_ in range(3):
            pa3, pa3n = next_ps("acc")
            for j in range(8):
                mm(pa3[0:NB_, :], hsT[:, j, :], W1[:, j, C_GQ + n_ * 512:C_GQ + (n_ + 1) * 512], j == 0, j == 7,
                   ["hT"] + w1res(C_GQ + n_ * 512, C_GQ + (n_ + 1) * 512), [pa3n], inc=(j == 7))
            cp("act" if n_ % 2 == 0 else "dve", bcum[0:NB_, :], pa3[0:NB_, :], [pa3n], ["bcum"])
            S.dma("sp", "os_cv_new", os_cv[:, 2, n_ * 512:(n_ + 1) * 512], bcum[0:NB_, :], reads=["bcum"])
            S.wait_all("sp", ["bcum"])
        S.dma("sp", "os_cv_pass", os_cv[:, 0:2, :], scv[:, 1:3, :], writes=["os_cv_pass_r"])
        outs_done.append("os_cv_pass_r")
        scv_f = scv.rearrange("b j c -> (b j) c")
        S.dma("sp", "o_sb0", o_sb[0][0:48, :], scv_f[:, 0:1024], writes=["o_sb0"])
        S.dma("sp", "o_sb1", o_sb[1][0:48, 0:512], scv_f[:, 1024:1536], writes=["o_sb1"])
        scvT = otmp[:, 0:576].rearrange("p (t q) -> p t q", q=48)
        for grp in range(2):
            pm_, pmn_ = next_ps("mx")
            cts = range(0, 8) if grp == 0 else range(8, 12)
            for ct in cts:
                src = o_sb[0][0:48, ct * 128:(ct + 1) * 128] if ct < 8 else o_sb[1][0:48, (ct - 8) * 128:(ct - 7) * 128]
                tr(pm_[:, (ct - cts[0]) * 48:(ct - cts[0] + 1) * 48], src, ident_f[0:48, 0:48], ["o_sb0", "o_sb1", "ident_f"], [pmn_], inc=(ct == cts[-1]))
            n_ct = len(cts)
            cp("dve", scvT[:, cts[0]:cts[0] + n_ct, :], pm_[:, 0:n_ct * 48].rearrange("p (t q) -> p t q", q=48), [pmn_, "otmp"], ["otmp"])
        scv4 = otmp[:, 0:576].rearrange("p (t b j) -> p t b j", b=NB_, j=3)
        cva = l2[:, 0:192].rearrange("p (t b) -> p t b", b=NB_)
        cvb = l2[:, 192:384].rearrange("p (t b) -> p t b", b=NB_)
        qkvc = dd[:, 0:192].rearrange("p (t b) -> p t b", b=NB_)
        tt(P, cva, scv4[:, :, :, 0], bc(cw_c[:, :, 0:1], [128, 12, NB_]), ALU.mult, ["otmp", "cw_c"], ["l2"])
        for j_ in (1, 2):
            tt(P, cvb, scv4[:, :, :, j_], bc(cw_c[:, :, j_:j_ + 1], [128, 12, NB_]), ALU.mult, ["otmp", "cw_c", "l2"], ["l2"])
            tt(P, cva, cva, cvb, ALU.add, ["l2"], ["l2"])
        tt(P, cvb, PT[:, 16:28, :], bc(cw_c[:, :, 3:4], [128, 12, NB_]), ALU.mult, ["e_sb", "cw_c", "l2"], ["l2"])
        tt(P, cva, cva, cvb, ALU.add, ["l2"], ["l2"])
        act(qkvc, cva, AF.Silu, ["l2"], ["dd"])
        szT = Sp_bf[:].rearrange("p h v -> p (h v)")[:, 0:8 * NB_].rearrange("p (j b) -> p j b", b=NB_)
        act(szT[:, 0:4, :], PT[:, 12:16, :], AF.Silu, ["e_sb"], ["Sp_bf"] + H4("Sp_bf"))
        act(szT[:, 4:8, :], PT[:, 28:32, :], AF.Silu, ["e_sb", "Sp_bf"], ["Sp_bf"])
        R1f = R1[:].rearrange("p h s -> p (h s)"); R2f = R2[:].rearrange("p h s -> p (h s)")
        R3f = R3[:].rearrange("p h s -> p (h s)"); R4f = R4[:].rearrange("p h s -> p (h s)")
        v3 = lambda ap: ap.rearrange("p (h b) -> p h b", b=NB_)
        e_s = v3(R1f[:, 0:64]); num_s = v3(R1f[:, 64:128]); den_s = v3(R1f[:, 128:192]); fg_s = v3(R1f[:, 192:256])
        kk_s = v3(R2f[:, 0:64])
        act(e_s, PT[:, 4:8, :], AF.Exp, ["e_sb"], ["R1"], scale=-1.0)
        tt("dve", num_s, e_s, bc(lb_c[:].unsqueeze(2), [128, 4, NB_]), ALU.mult, ["R1", "lb_c"], ["R1"])
        ts("dve", num_s, num_s, 1.0, None, ALU.add, None, ["R1"], ["R1"])
        ts("dve", den_s, e_s, 1.0, None, ALU.add, None, ["R1"], ["R1"])
        S.op("dve", "reciprocal", den_s, den_s, reads=["R1"], writes=["R1"])
        tt("dve", fg_s, num_s, den_s, ALU.mult, ["R1"], ["R1"])
        ts("dve", kk_s, fg_s, -1.0, 1.0, ALU.mult, ALU.add, ["R1"], ["R2"])
        sq_s = R2f[:, 64:192]
        tt("dve", sq_s, dd[:, 0:128], dd[:, 0:128], ALU.mult, ["dd", "R2"], ["R2"])
        pm_, pmn_ = next_ps("mx")
        mm(pm_[:, 0:128], ones_f[:], sq_s, True, True, ["ones_f", "R2"], [pmn_])
        rn_s = R3f[:, 0:128]
        act(rn_s, pm_[:, 0:128], AF.Ln, [pmn_, "eps_c"], ["R3"], bias=eps_c[:, :])
        act(rn_s, rn_s, AF.Exp, ["R3"], ["R3"], scale=-0.5)
        qkn = R3f[:, 128:256]
        tt("dve", qkn, dd[:, 0:128], rn_s, ALU.mult, ["dd", "R3"], ["R3"])
        ts("dve", qkn[:, 0:64], qkn[:, 0:64], float(128.0 ** -0.5), None, ALU.mult, None, ["R3"], ["R3"])
        qn_s = v3(qkn[:, 0:64]); kn_s = v3(qkn[:, 64:128]); vc_s = qkvc[:, 8:12, :]
        rall = R4f[0:8, 0:128].rearrange("p (r b) -> p r b", b=NB_)
        tt("dve", rall, bc(bgT.unsqueeze(1), [8, 8, NB_]), bc(ident_f[0:8, 0:8].unsqueeze(2), [8, 8, NB_]), ALU.mult, ["l1", "ident_f"], ["R4"])
        pm2, pm2n = next_ps("mx")
        mm(pm2[:, 0:128], ones_f[0:8, :], R4f[0:8, 0:128], True, True, ["ones_f", "R4"], [pm2n])
        bgb = tmpA[:].rearrange("p h s -> p (h s)")
        beta_s = v3(bgb[:, 0:64]); eg_s = v3(bgb[:, 64:128]); tsm = v3(bgb[:, 128:192])
        act(beta_s, pm2[:, 0:64].rearrange("p (h b) -> p h b", b=NB_), AF.Exp, [pm2n], ["tmpA"], scale=-1.0)
        act(beta_s, beta_s, AF.Ln, ["tmpA"], ["tmpA"], bias=1.0)
        act(beta_s, beta_s, AF.Exp, ["tmpA"], ["tmpA"], scale=-1.0)
        tt("dve", tsm, pm2[:, 64:128].rearrange("p (h b) -> p h b", b=NB_), bc(dtb_bc[:].unsqueeze(2), [128, 4, NB_]), ALU.add, [pm2n, "dtb_bc", "tmpA"], ["tmpA"])
        act(tsm, tsm, AF.Exp, ["tmpA"], ["tmpA"])
        act(tsm, tsm, AF.Ln, ["tmpA"], ["tmpA"], bias=1.0)
        tt("dve", tsm, tsm, bc(negA_bc[:].unsqueeze(2), [128, 4, NB_]), ALU.mult, ["tmpA", "negA_bc"], ["tmpA"])
        act(eg_s, tsm, AF.Exp, ["tmpA"], ["tmpA"])
        Shg = [S_hg, LL[:].rearrange("p (h v) -> p h v", h=4), cacc[1][:].rearrange("p (h v) -> p h v", h=4)]; Shn = ["S_hg", "LL", "cacc1"]
        Sgd = [S_gd, cacc[0][:].rearrange("p (h v) -> p h v", h=4), otmp[:, 512:1024].rearrange("p (h v) -> p h v", h=4)]; Sgn = ["S_gd", "cacc0", "otmp"]
        diag4 = pre[0][:, 0:512].rearrange("p (h v) -> p h v", h=4); tbuf = pre[1][:, 0:512].rearrange("p (h v) -> p h v", h=4)
        diag4G = o_sb[0][:, 0:512].rearrange("p (h v) -> p h v", h=4); tbufG = o_sb[1][:, 0:512].rearrange("p (h v) -> p h v", h=4)
        dHh = qTt[:, :, 0:128]; dHl = qTt[:, :, 128:256]; dGh = kTt[:, :, 0:128]; dGl = kTt[:, :, 128:256]
        vhA = vcT[:, :, 0:NB_]; vlA = vcT[:, :, NB_:2 * NB_]; dh1 = vcT[:, :, 2 * NB_:2 * NB_ + 1]; dl1 = vcT[:, :, 2 * NB_ + 1:2 * NB_ + 2]
        idbb = bc(ident_bf[:].unsqueeze(1), [128, 4, 128])
        dlt = v3(tmpB[:].rearrange("p h s -> p (h s)")[:, 0:64])
        idb = bc(ident_f[:].unsqueeze(1), [128, 4, 128])
        hq_s = PT[:, 0:4, :]; hi_s = PT[:, 8:12, :]
        cp("dve", vhA, hi_s, ["e_sb"], H4("vcT"))
        tt("dve", vlA, hi_s, vhA, ALU.subtract, ["e_sb"] + H4("vcT"), H4("vcT"))

        def sview(t, k):
            return t[k][:] if k == 0 else t[k]
        def load_states(b):
            k_ = b % 3
            S.dma("sp", "in_" + Shn[k_], sview(Shg, k_), shg[b].rearrange("h k v -> k h v"), writes=[Shn[k_]])
            S.dma("act", "in_" + Sgn[k_], sview(Sgd, k_), sgd[b].rearrange("h k v -> k h v"), writes=[Sgn[k_]])

        def tok_bufs(b):
            k_ = b % 3
            bH, bHn = (rcA, "rcA") if b % 2 == 0 else (pp[1], "pp1")
            bG, bGn = (rcB, "rcB") if b % 2 == 0 else (acc[1], "acc1")
            return sview(Shg, k_), Shn[k_], sview(Sgd, k_), Sgn[k_], bH, bHn, bG, bGn

        def phase1(b):
            sh, shn, sg, sgn, bH, bHn, bG, bGn = tok_bufs(b)
            pk_, pkn_ = acc[0], "acc0"
            for h in range(4):
                mm(pk_[:, h:h + 1], sg[:, h, :], kn_s[:, h, b:b + 1], True, True, [sgn, "R3"], [pkn_], inc=(h == 3))
            tt(P, dHh, idbb, bc(vhA[:, :, b:b + 1], [128, 4, 128]), ALU.mult, ["ident_bf"] + H4("vcT"), H4("qTt"))
            tt(P, dHl, idbb, bc(vlA[:, :, b:b + 1], [128, 4, 128]), ALU.mult, ["ident_bf"] + H4("vcT") + H4("qTt"), H4("qTt"))
            tt(P, sh, sh, bc(fg_s[:, :, b:b + 1], [128, 4, 128]), ALU.mult, [shn, "R1"], [shn])
            for h in range(4):
                mm(bH[:, h * 128:(h + 1) * 128], ones_bf[:], dHh[:, h, :], True, False, ["ones_bf"] + H4("qTt"), [bHn], inc=False)
                mm(bH[:, h * 128:(h + 1) * 128], ones_bf[:], dHl[:, h, :], False, True, ["ones_bf"] + H4("qTt"), [bHn], inc=(h == 3))
            tt("dve", dlt[:, :, 0], pk_[:, 0:4], eg_s[:, :, b], ALU.mult, [pkn_, "tmpA"], ["tmpB"])
            tt("dve", dlt[:, :, 0], vc_s[:, :, b], dlt[:, :, 0], ALU.subtract, ["dd", "tmpB"], ["tmpB"])
            tt("dve", dlt[:, :, 0], dlt[:, :, 0], beta_s[:, :, b], ALU.mult, ["tmpB", "tmpA"], ["tmpB"])
            cp("dve", dh1, dlt[:, :, 0:1], ["tmpB"] + H4("vcT"), H4("vcT"))
            tt("dve", dl1, dlt[:, :, 0:1], dh1, ALU.subtract, ["tmpB"] + H4("vcT"), H4("vcT"))
            tt(P, dGh, idbb, bc(dh1, [128, 4, 128]), ALU.mult, ["ident_bf"] + H4("vcT"), H4("kTt"))
            tt(P, dGl, idbb, bc(dl1, [128, 4, 128]), ALU.mult, ["ident_bf"] + H4("vcT") + H4("kTt"), H4("kTt"))
            tt(P, sg, sg, bc(eg_s[:, :, b:b + 1], [128, 4, 128]), ALU.mult, [sgn, "tmpA"], [sgn])
            for h in range(4):
                mm(bG[:, h * 128:(h + 1) * 128], ones_bf[:], dGh[:, h, :], True, False, ["ones_bf"] + H4("kTt"), [bGn], inc=False)
                mm(bG[:, h * 128:(h + 1) * 128], ones_bf[:], dGl[:, h, :], False, True, ["ones_bf"] + H4("kTt"), [bGn], inc=(h == 3))

        def phase2(b):
            sh, shn, sg, sgn, bH, bHn, bG, bGn = tok_bufs(b)
            tt("dve", tbuf, bH[:, :].rearrange("p (h v) -> p h v", h=4), bc(kk_s[:, :, b:b + 1], [128, 4, 128]), ALU.mult, [bHn, "R2"], ["pre1"])
            tt("dve", sh, sh, tbuf, ALU.add, [shn, "pre1"], [shn])
            S.dma("sp", "out_" + shn, os_hg[b].rearrange("h k v -> k h v"), sh, reads=[shn])
            for h in range(4):
                mm(pp[0][:, h * NB_ + b:h * NB_ + b + 1], sh[:, h, :], hq_s[:, h, b:b + 1], True, True, [shn, "e_sb"], ["pp0"], inc=(h == 3))
            tt("dve", tbufG, bG[:, :].rearrange("p (h v) -> p h v", h=4), bc(kn_s[:, :, b:b + 1], [128, 4, 128]), ALU.mult, [bGn, "R3"], ["o_sb1"])
            tt("dve", sg, sg, tbufG, ALU.add, [sgn, "o_sb1"], [sgn])
            S.dma("act", "out_" + sgn, os_gd[b].rearrange("h k v -> k h v"), sg, reads=[sgn])
            for h in range(4):
                mm(pp[0][:, 64 + h * NB_ + b:64 + h * NB_ + b + 1], sg[:, h, :], qn_s[:, h, b:b + 1], True, True, [sgn, "R3"], ["pp0"], inc=(h == 3))

        load_states(0)
        load_states(1)
        phase1(0)
        for b in range(NB_):
            if b + 2 < NB_:
                load_states(b + 2)
            if b + 1 < NB_:
                phase1(b + 1)
            phase2(b)
        outs_done += Shn + Sgn
        oT_s = bcum[:, 0:128]
        cp("dve", oT_s, pp[0][:, 0:128], ["pp0"], ["bcum"])
        sq2 = bcum[:, 128:256]
        tt("dve", sq2, oT_s, oT_s, ALU.mult, ["bcum"], ["bcum"])
        pm3, pm3n = next_ps("mx")
        mm(pm3[:, 0:128], ones_f[:], sq2, True, True, ["ones_f", "bcum"], [pm3n])
        rs2 = bcum[:, 256:384]
        act(rs2, pm3[:, 0:128], AF.Ln, [pm3n, "eps_c"], ["bcum"], bias=eps_c[:, :], scale=1.0 / 128)
        act(rs2, rs2, AF.Exp, ["bcum"], ["bcum"], scale=-0.5)
        tt("dve", oT_s, oT_s, rs2, ALU.mult, ["bcum"], ["bcum"])
        yTs = u_bf[:].rearrange("p h v -> p (h v)")[:, 0:128]
        tt("dve", yTs, oT_s, szT.rearrange("p j b -> p (j b)"), ALU.mult, ["bcum", "Sp_bf"], ["u_bf"])
        yT3 = yTs.rearrange("p (j b) -> p j b", b=NB_)
        for half in range(2):
            pa, pan = next_ps("acc")
            for j in range(8):
                mm(pa[0:NB_, :], yT3[:, j, :], W2[:, j, half * 512:(half + 1) * 512], j == 0, j == 7, ["u_bf", "W2.%d" % j], [pan], inc=(j == 7))
            tt("dve", xr[0:NB_, half * 512:(half + 1) * 512], pa[0:NB_, :], xr[0:NB_, half * 512:(half + 1) * 512], ALU.add, [pan, "xr"], ["xr"])
        act(ybf[0:NB_, :], xr[0:NB_, :], AF.Square, ["xr"], ["ybf", "ssqf"], accum_out=ssqf[0:NB_, :])
        rstd_from_ssq(ssqf[0:NB_, :], NB_, D, "ssqf")
        stt(xr[0:NB_, :], xr[0:NB_, :], ssqf[0:NB_, :], fin_bc[0:NB_, :], ALU.mult, ALU.mult, ["xr", "ssqf", "fin_bc"], ["xr"])
        S.dma("sp", "ys", ys, xr[0:NB_, :], reads=["xr"])
        outs_done += ["xr"]

        S.wait_all("sp", outs_done)
        S.emit()
    return nc


_NC_CACHE = {}


def kernel(x_prompt, x_sample, state_hgrn, state_gdn, state_gdn_conv, norm_w, w_in, hg_lb_logits, conv_w,
           gdn_a_log, gdn_dt_bias, hg_out_norm, gdn_out_norm, w_out, final_norm, _dbg=None):
    f = lambda a: np.ascontiguousarray(np.asarray(a, dtype=np.float32))
    key = tuple(sorted(_dbg)) if _dbg else ()
    if key not in _NC_CACHE:
        _NC_CACHE[key] = build(_dbg)
    nc = _NC_CACHE[key]
    in_maps = []
    for c in range(NCORE):
        sl = slice(c * NS, (c + 1) * NS)
        in_maps.append({
            "xp": f(x_prompt[c]), "xs": f(x_sample[sl, 0]),
            "shg": f(state_hgrn[0, sl]), "sgd": f(state_gdn[0, sl]), "scv": f(state_gdn_conv[0, sl]),
            "norm_w": f(norm_w[0]), "w_in": f(w_in[0]), "lbl": f(hg_lb_logits), "conv_w": f(conv_w[0]),
            "a_log": f(gdn_a_log[0]), "dt_bias": f(gdn_dt_bias[0]), "hgn": f(hg_out_norm[0]), "gdnn": f(gdn_out_norm[0]),
            "w_out": f(w_out[0]), "fin": f(final_norm),
        })
    res = run_bass_kernel_spmd(nc, in_maps, core_ids=list(range(NCORE)))
    r = res.results
    if _dbg:
        return r
    y_prompt = np.stack([r[c]["yp"] for c in range(NCORE)], 0)
    y_sample = np.concatenate([r[c]["ys"] for c in range(NCORE)], 0)[:, None, :]
    nhp = np.stack([r[c]["o_shg"] for c in range(NCORE)], 0)[None]
    ngp = np.stack([r[c]["o_sgd"] for c in range(NCORE)], 0)[None]
    ncp = np.stack([r[c]["o_cv"] for c in range(NCORE)], 0)[None]
    nhs = np.concatenate([r[c]["os_hg"] for c in range(NCORE)], 0)[None]
    ngs = np.concatenate([r[c]["os_gd"] for c in range(NCORE)], 0)[None]
    ncs = np.concatenate([r[c]["os_cv"] for c in range(NCORE)], 0)[None]
    return (y_prompt, y_sample, nhp, ngp, ncp, nhs, ngs, ncs)
```
